# Optimizing a Trainium2 kernel written in Bass

```python
import math
import jax, jax.numpy as jnp
from jax import lax
import numpy as np

D_MODEL = 4096
BATCH = 4
SEQ = 2048
DEPTH = 2

D_MIX = D_MODEL // 2
N_BRANCH = 3
HY_WIDTH = D_MIX
HY_ORDER = 2
HY_SHORT = 3
HY_EMB = 33
HY_BANDS = (HY_EMB - 1) // 2
HY_FILTER_HIDDEN = 64
HY_N_INNER = 2
HY_FAST_DECAY = 0.3
HY_SLOW_DECAY = 1.5
HY_DECAY_TARGET = 1e-2
ML_HEADS = 8
ML_DV = D_MIX // ML_HEADS
ML_DK = ML_DV // 2
ML_CHUNK = 64
ML_FGATE_LO = 3.0
ML_FGATE_HI = 6.0
RG_WIDTH = D_MIX
RG_HEADS = 8
RG_BLOCK = RG_WIDTH // RG_HEADS
RG_CONV = 4
RG_C = 8.0
FFN_HIDDEN = ((8 * D_MODEL // 3 + 255) // 256) * 256
EPS = 1e-6
IN_SIZES = (
    (HY_ORDER + 1) * HY_WIDTH,
    ML_HEADS * ML_DK,
    ML_HEADS * ML_DK,
    ML_HEADS * ML_DV,
    ML_HEADS * ML_DV,
    4 * ML_HEADS,
    RG_WIDTH,
    RG_WIDTH,
    N_BRANCH * D_MODEL,
)
D_IN = sum(IN_SIZES)

kernel_name = 'hybrid_hyena_mlstm_rglru_encoder'


def _rms_norm(x, g):
    x32 = x.astype(jnp.float32)
    y = x32 * lax.rsqrt(jnp.mean(x32 * x32, axis=-1, keepdims=True) + EPS)
    return (y * g.astype(jnp.float32)).astype(x.dtype)


def _depthwise_conv(x, w, b, left):
    k = w.shape[0]
    c = x.shape[-1]
    y = lax.conv_general_dilated(
        x, w[:, None, :].astype(x.dtype), window_strides=(1,),
        padding=[(left, k - 1 - left)],
        dimension_numbers=('NWC', 'WIO', 'NWC'), feature_group_count=c)
    return y + b.astype(x.dtype)


def _flip(t):
    return jnp.flip(t, axis=1)


def _split_columns(p):
    offs = []
    acc = 0
    for s in IN_SIZES[:-1]:
        acc += s
        offs.append(acc)
    return jnp.split(p, offs, axis=-1)


def _hyena_filters(seq_len, w1, b1, w2, b2, freq, w3):
    f32 = jnp.float32
    pos = jnp.arange(seq_len, dtype=f32)
    t = pos / (seq_len - 1)
    omega = 2.0 * math.pi * pos / seq_len
    bands = jnp.linspace(1e-4, HY_BANDS - 1, HY_BANDS, dtype=f32)
    ang = omega[:, None] * bands[None, :]
    feat = jnp.concatenate([t[:, None], jnp.cos(ang), -jnp.sin(ang)], axis=-1)
    freq = freq.astype(f32)
    hdn = jnp.sin(freq * (feat @ w1.astype(f32) + b1.astype(f32)))
    for j in range(HY_N_INNER):
        hdn = jnp.sin(freq * (hdn @ w2[j].astype(f32) + b2[j].astype(f32)))
    filt = (hdn @ w3.astype(f32)).reshape(seq_len, HY_ORDER, 2, HY_WIDTH)
    deltas = jnp.abs(jnp.linspace(math.log(HY_DECAY_TARGET) / HY_SLOW_DECAY,
                                  math.log(HY_DECAY_TARGET) / HY_FAST_DECAY,
                                  HY_WIDTH, dtype=f32))
    window = jnp.exp(-t[:, None] * deltas[None, :])
    filt = filt * window[:, None, None, :]
    fwd = filt[:, :, 0]
    bwd = filt[:, :, 1]
    kern = jnp.concatenate(
        [fwd, jnp.zeros((1, HY_ORDER, HY_WIDTH), f32), bwd[:0:-1]], axis=0)
    kern = kern * lax.rsqrt(jnp.sum(kern * kern, axis=0, keepdims=True))
    return jnp.fft.rfft(kern, axis=0)


def _fft_conv(z, kf):
    seq_len = z.shape[1]
    zf = jnp.fft.rfft(z.astype(jnp.float32), n=2 * seq_len, axis=1)
    y = jnp.fft.irfft(zf * kf[None], n=2 * seq_len, axis=1)[:, :seq_len]
    return y.astype(z.dtype)


def _hyena_branch(u, conv_w, conv_b, w1, b1, w2, b2, freq, w3, skip):
    u = _depthwise_conv(u, conv_w, conv_b, (HY_SHORT - 1) // 2)
    x1, x2, v = jnp.split(u, HY_ORDER + 1, axis=-1)
    kf = _hyena_filters(u.shape[1], w1, b1, w2, b2, freq, w3)
    z = x1 * (_fft_conv(v, kf[:, 0]) + skip[0].astype(v.dtype) * v)
    return x2 * (_fft_conv(z, kf[:, 1]) + skip[1].astype(z.dtype) * z)


def _mlstm_chunkwise(q, k, v, i_pre, log_f):
    f32 = jnp.float32
    bsz, seq_len, nh, dk = q.shape
    dv = v.shape[-1]
    nc = seq_len // ML_CHUNK

    def chunks4(t):
        return t.astype(f32).reshape(bsz, nc, ML_CHUNK, nh, t.shape[-1]).transpose(1, 0, 3, 2, 4)

    def chunks3(t):
        return t.astype(f32).reshape(bsz, nc, ML_CHUNK, nh).transpose(1, 0, 3, 2)

    xs = (chunks4(q), chunks4(k), chunks4(v), chunks3(i_pre), chunks3(log_f))
    mask = jnp.tril(jnp.ones((ML_CHUNK, ML_CHUNK), dtype=bool))

    def step(carry, inp):
        c_st, n_st, m_st = carry
        qc, kc, vc, ic, fc = inp
        b = jnp.cumsum(fc, axis=-1)
        g = b[..., -1]
        dmat = b[..., :, None] - b[..., None, :] + ic[..., None, :]
        dmat = jnp.where(mask, dmat, -jnp.inf)
        inter = b + m_st[..., None]
        m_j = jnp.maximum(inter, jnp.max(dmat, axis=-1))
        w_intra = jnp.exp(dmat - m_j[..., None])
        w_inter = jnp.exp(inter - m_j)
        s = jnp.einsum('bhjd,bhld->bhjl', qc, kc) * w_intra
        num = (w_inter[..., None] * jnp.einsum('bhvd,bhjd->bhjv', c_st, qc)
               + jnp.einsum('bhjl,bhlv->bhjv', s, vc))
        den = w_inter * jnp.einsum('bhd,bhjd->bhj', n_st, qc) + jnp.sum(s, axis=-1)
        h = num / jnp.maximum(jnp.abs(den), jnp.exp(-m_j))[..., None]
        lw = g[..., None] - b + ic
        m_new = jnp.maximum(g + m_st, jnp.max(lw, axis=-1))
        wl = jnp.exp(lw - m_new[..., None])
        decay = jnp.exp(g + m_st - m_new)
        c_new = decay[..., None, None] * c_st + jnp.einsum('bhl,bhlv,bhld->bhvd', wl, vc, kc)
        n_new = decay[..., None] * n_st + jnp.einsum('bhl,bhld->bhd', wl, kc)
        return (c_new, n_new, m_new), h

    init = (jnp.zeros((bsz, nh, dv, dk), f32), jnp.zeros((bsz, nh, dk), f32),
            jnp.zeros((bsz, nh), f32))
    _, hs = lax.scan(step, init, xs)
    return hs.transpose(1, 0, 3, 2, 4).reshape(bsz, seq_len, nh, dv)


def _mlstm_branch(q, k, v, o, gates, gate_b, norm_w):
    bsz, seq_len, _ = q.shape
    q = q.reshape(bsz, seq_len, ML_HEADS, ML_DK)
    k = k.reshape(bsz, seq_len, ML_HEADS, ML_DK) * (ML_DK ** -0.5)
    v = v.reshape(bsz, seq_len, ML_HEADS, ML_DV)
    gpre = gates.reshape(bsz, seq_len, 2, 2, ML_HEADS).astype(jnp.float32) + gate_b.astype(jnp.float32)
    i_pre = gpre[:, :, :, 0]
    log_f = jax.nn.log_sigmoid(gpre[:, :, :, 1])
    h_fwd = _mlstm_chunkwise(q, k, v, i_pre[:, :, 0], log_f[:, :, 0])
    h_bwd = _flip(_mlstm_chunkwise(_flip(q), _flip(k), _flip(v),
                                   _flip(i_pre[:, :, 1]), _flip(log_f[:, :, 1])))
    hsum = h_fwd + h_bwd
    hn = hsum * lax.rsqrt(jnp.mean(hsum * hsum, axis=-1, keepdims=True) + EPS)
    hn = hn.reshape(bsz, seq_len, ML_HEADS * ML_DV) * norm_w.astype(jnp.float32)
    return (jax.nn.sigmoid(o.astype(jnp.float32)) * hn).astype(o.dtype)


def _rglru_scan(xc, wa, ba, wx, bx, lam):
    f32 = jnp.float32
    bsz, seq_len, c = xc.shape
    x32 = xc.astype(f32)
    xh = x32.reshape(bsz, seq_len, RG_HEADS, RG_BLOCK)
    r = jax.nn.sigmoid(jnp.einsum('blhi,hij->blhj', xh, wa.astype(f32)).reshape(bsz, seq_len, c)
                       + ba.astype(f32))
    ig = jax.nn.sigmoid(jnp.einsum('blhi,hij->blhj', xh, wx.astype(f32)).reshape(bsz, seq_len, c)
                        + bx.astype(f32))
    log_a = -RG_C * r * jax.nn.softplus(-lam.astype(f32))
    a = jnp.exp(log_a)
    bterm = jnp.sqrt(-jnp.expm1(2.0 * log_a)) * (ig * x32)

    def comb(e1, e2):
        a1, b1 = e1
        a2, b2 = e2
        return a1 * a2, a2 * b1 + b2

    _, h = lax.associative_scan(comb, (a, bterm), axis=1)
    return h


def _rglru_branch(xr, yr, conv_w, conv_b, wa, ba, wx, bx, lam):
    xc = _depthwise_conv(xr, conv_w, conv_b, RG_CONV // 2)
    h_fwd = _rglru_scan(xc, wa[0], ba[0], wx[0], bx[0], lam[0])
    h_bwd = _flip(_rglru_scan(_flip(xc), wa[1], ba[1], wx[1], bx[1], lam[1]))
    return ((h_fwd + h_bwd) * jax.nn.gelu(yr.astype(jnp.float32))).astype(xr.dtype)


def setup_inputs(seed: int = 0) -> dict:
    key = jax.random.key(seed)
    ks = jax.random.split(key, 32)
    f32 = jnp.float32

    def nrm(k, shape, scale):
        return jax.random.normal(k, shape, f32) * scale

    x = nrm(ks[0], (BATCH, SEQ, D_MODEL), 1.0)
    mix_norm = 1.0 + nrm(ks[1], (DEPTH, D_MODEL), 0.05)
    w_in = nrm(ks[2], (DEPTH, D_MODEL, D_IN), D_MODEL ** -0.5)
    b_gate = nrm(ks[3], (DEPTH, N_BRANCH * D_MODEL), 0.01)
    hy_conv_w = nrm(ks[4], (DEPTH, HY_SHORT, (HY_ORDER + 1) * HY_WIDTH), HY_SHORT ** -0.5)
    hy_conv_b = nrm(ks[5], (DEPTH, (HY_ORDER + 1) * HY_WIDTH), 0.01)
    hy_w1 = nrm(ks[6], (DEPTH, HY_EMB, HY_FILTER_HIDDEN), HY_EMB ** -0.5)
    hy_b1 = nrm(ks[7], (DEPTH, HY_FILTER_HIDDEN), 0.01)
    hy_w2 = nrm(ks[8], (DEPTH, HY_N_INNER, HY_FILTER_HIDDEN, HY_FILTER_HIDDEN), HY_FILTER_HIDDEN ** -0.5)
    hy_b2 = nrm(ks[9], (DEPTH, HY_N_INNER, HY_FILTER_HIDDEN), 0.01)
    hy_freq = 1.0 + nrm(ks[10], (DEPTH, HY_FILTER_HIDDEN), 0.05)
    hy_w3 = nrm(ks[11], (DEPTH, HY_FILTER_HIDDEN, HY_ORDER * 2 * HY_WIDTH), HY_FILTER_HIDDEN ** -0.5)
    hy_skip = nrm(ks[12], (DEPTH, HY_ORDER, HY_WIDTH), 0.1)
    ig_b = nrm(ks[13], (DEPTH, 2, 1, ML_HEADS), 0.01)
    fg_b = (jnp.linspace(ML_FGATE_LO, ML_FGATE_HI, ML_HEADS, dtype=f32)[None, None, None, :]
            + nrm(ks[14], (DEPTH, 2, 1, ML_HEADS), 0.01))
    ml_gate_b = jnp.concatenate([ig_b, fg_b], axis=2)
    ml_norm = 1.0 + nrm(ks[15], (DEPTH, ML_HEADS * ML_DV), 0.05)
    rg_conv_w = nrm(ks[16], (DEPTH, RG_CONV, RG_WIDTH), RG_CONV ** -0.5)
    rg_conv_b = nrm(ks[17], (DEPTH, RG_WIDTH), 0.01)
    rg_wa = nrm(ks[18], (DEPTH, 2, RG_HEADS, RG_BLOCK, RG_BLOCK), RG_BLOCK ** -0.5)
    rg_ba = nrm(ks[19], (DEPTH, 2, RG_WIDTH), 0.01)
    rg_wx = nrm(ks[20], (DEPTH, 2, RG_HEADS, RG_BLOCK, RG_BLOCK), RG_BLOCK ** -0.5)
    rg_bx = nrm(ks[21], (DEPTH, 2, RG_WIDTH), 0.01)
    a0 = jax.random.uniform(ks[22], (DEPTH, 2, RG_WIDTH), f32, 0.9, 0.999)
    s0 = a0 ** (1.0 / RG_C)
    rg_lambda = jnp.log(s0) - jnp.log1p(-s0)
    w_br_a = nrm(ks[23], (DEPTH, HY_WIDTH, D_MODEL), HY_WIDTH ** -0.5)
    w_br_b = nrm(ks[24], (DEPTH, ML_HEADS * ML_DV, D_MODEL), (ML_HEADS * ML_DV) ** -0.5)
    w_br_c = nrm(ks[25], (DEPTH, RG_WIDTH, D_MODEL), RG_WIDTH ** -0.5)
    w_out = nrm(ks[26], (DEPTH, D_MODEL, D_MODEL), D_MODEL ** -0.5)
    ffn_norm = 1.0 + nrm(ks[27], (DEPTH, D_MODEL), 0.05)
    w_gate = nrm(ks[28], (DEPTH, D_MODEL, FFN_HIDDEN), D_MODEL ** -0.5)
    w_up = nrm(ks[29], (DEPTH, D_MODEL, FFN_HIDDEN), D_MODEL ** -0.5)
    w_down = nrm(ks[30], (DEPTH, FFN_HIDDEN, D_MODEL), FFN_HIDDEN ** -0.5)
    final_norm = 1.0 + nrm(ks[31], (D_MODEL,), 0.05)
    return {'x': x, 'mix_norm': mix_norm, 'w_in': w_in, 'b_gate': b_gate,
            'hy_conv_w': hy_conv_w, 'hy_conv_b': hy_conv_b, 'hy_w1': hy_w1, 'hy_b1': hy_b1,
            'hy_w2': hy_w2, 'hy_b2': hy_b2, 'hy_freq': hy_freq, 'hy_w3': hy_w3, 'hy_skip': hy_skip,
            'ml_gate_b': ml_gate_b, 'ml_norm': ml_norm,
            'rg_conv_w': rg_conv_w, 'rg_conv_b': rg_conv_b, 'rg_wa': rg_wa, 'rg_ba': rg_ba,
            'rg_wx': rg_wx, 'rg_bx': rg_bx, 'rg_lambda': rg_lambda,
            'w_br_a': w_br_a, 'w_br_b': w_br_b, 'w_br_c': w_br_c, 'w_out': w_out,
            'ffn_norm': ffn_norm, 'w_gate': w_gate, 'w_up': w_up, 'w_down': w_down,
            'final_norm': final_norm}


def reference(x, mix_norm, w_in, b_gate, hy_conv_w, hy_conv_b, hy_w1, hy_b1, hy_w2, hy_b2,
              hy_freq, hy_w3, hy_skip, ml_gate_b, ml_norm, rg_conv_w, rg_conv_b, rg_wa, rg_ba,
              rg_wx, rg_bx, rg_lambda, w_br_a, w_br_b, w_br_c, w_out, ffn_norm, w_gate, w_up,
              w_down, final_norm):
    bsz, seq_len, _ = x.shape
    h = x
    for l in range(DEPTH):
        u = _rms_norm(h, mix_norm[l])
        proj = jnp.einsum('bld,dn->bln', u, w_in[l])
        hy_in, ml_q, ml_k, ml_v, ml_o, ml_g, rg_x, rg_y, g = _split_columns(proj)
        y_a = _hyena_branch(hy_in, hy_conv_w[l], hy_conv_b[l], hy_w1[l], hy_b1[l], hy_w2[l],
                            hy_b2[l], hy_freq[l], hy_w3[l], hy_skip[l])
        y_b = _mlstm_branch(ml_q, ml_k, ml_v, ml_o, ml_g, ml_gate_b[l], ml_norm[l])
        y_c = _rglru_branch(rg_x, rg_y, rg_conv_w[l], rg_conv_b[l], rg_wa[l], rg_ba[l],
                            rg_wx[l], rg_bx[l], rg_lambda[l])
        gate = jax.nn.sigmoid((g + b_gate[l]).reshape(bsz, seq_len, N_BRANCH, D_MODEL))
        merged = (gate[:, :, 0] * jnp.einsum('blc,cd->bld', y_a, w_br_a[l])
                  + gate[:, :, 1] * jnp.einsum('blc,cd->bld', y_b, w_br_b[l])
                  + gate[:, :, 2] * jnp.einsum('blc,cd->bld', y_c, w_br_c[l]))
        h = h + jnp.einsum('bld,de->ble', merged, w_out[l])
        u = _rms_norm(h, ffn_norm[l])
        ff = jax.nn.silu(jnp.einsum('bld,df->blf', u, w_gate[l])) * jnp.einsum('bld,df->blf', u, w_up[l])
        h = h + jnp.einsum('blf,fd->bld', ff, w_down[l])
    return _rms_norm(h, final_norm)
```

```python
import contextlib
import math
import numpy as np
import concourse.bass as bass
import concourse.mybir as mybir
from concourse.bass_utils import run_bass_kernel_spmd

F32 = mybir.dt.float32
BF16 = mybir.dt.bfloat16
AF = mybir.ActivationFunctionType
ALU = mybir.AluOpType
AX = mybir.AxisListType

D = 4096
SEQ = 2048
DEPTH = 2
DMIX = 2048
NHEAD = 8
DK = 128
DV = 256
FFN = 11008
D_IN = 28704
EPS = 1e-6
N_DMA_SEMS = 12
MAGIC = 12582912.0
TWO_PI = float(2 * np.pi)

C_HY = 0
C_Q = 6144
C_K = 7168
C_V = 8192
C_O = 10240
C_MG = 12288
C_RX = 12320
C_RY = 14368
C_G = 16416


class Tile:
    def __init__(self, name, t):
        self.name = name
        self.t = t

    def __getitem__(self, idx):
        return self.t[idx]

    def k(self, *idx):
        return (self.name,) + tuple(idx)


def _key(x):
    if isinstance(x, Tile):
        return (x.name,)
    if isinstance(x, tuple):
        return x
    return (x,)


class Prog:
    ENG = ("pe", "dve", "act", "pool", "sp")

    def __init__(self, nc):
        self.nc = nc
        self.stack = contextlib.ExitStack()
        self.tstack = None
        self.ops = {e: [] for e in self.ENG}
        self.cnt = {}
        self.sems = {}
        self.known = {e: {} for e in self.ENG}
        self.bufs = {}
        for e in self.ENG:
            self.sems[e] = self.stack.enter_context(nc.semaphore("s_" + e))
            self.cnt[e] = 0
        self.dma_names = {}
        self.dma_rr = {}
        for q in ("sp", "pool", "act"):
            names = []
            for i in range(N_DMA_SEMS):
                n = "d_%s%d" % (q, i)
                self.sems[n] = self.stack.enter_context(nc.semaphore(n))
                self.cnt[n] = 0
                names.append(n)
            self.dma_names[q] = names
            self.dma_rr[q] = 0
        self.nphase = 0
        self.uid = 0
        self.psum_names = set()

    def begin(self):
        self.tstack = contextlib.ExitStack()
        if self.nphase > 0:
            for e in self.ENG:
                for s, v in self.cnt.items():
                    if v > 0 and not (s == e and e == "pe") and self.known[e].get(s, 0) < v:
                        self.ops[e].append(("wait", s, v))
                        self.known[e][s] = v
            self.bufs = {}
        self.nphase += 1

    def end(self, final=False):
        nc = self.nc
        if final:
            for s, v in self.cnt.items():
                if s != "sp" and v > 0 and self.known["sp"].get(s, 0) < v:
                    self.ops["sp"].append(("wait", s, v))
        ops = self.ops
        sems = self.sems
        with nc.Block() as block:
            def run(eng, name):
                for o in ops[name]:
                    if o[0] == "wait":
                        eng.wait_ge(sems[o[1]], o[2])
                    else:
                        o[1](eng).then_inc(sems[o[2]], o[3])

            @block.tensor
            def _(e):
                run(e, "pe")

            @block.vector
            def _(e):
                run(e, "dve")

            @block.scalar
            def _(e):
                run(e, "act")

            @block.gpsimd
            def _(e):
                run(e, "pool")

            @block.sync
            def _(e):
                run(e, "sp")
        self.ops = {e: [] for e in self.ENG}
        self.tstack.close()
        self.tstack = None
        if final:
            self.stack.close()

    def sb(self, name, shape, dtype):
        self.uid += 1
        nm = "%s_%d" % (name, self.uid)
        return Tile(nm, self.tstack.enter_context(self.nc.sbuf_tensor(nm, list(shape), dtype)))

    def ps(self, name, shape, dtype=F32):
        self.uid += 1
        nm = "%s_%d" % (name, self.uid)
        self.psum_names.add(nm)
        return Tile(nm, self.tstack.enter_context(self.nc.psum_tensor(nm, list(shape), dtype)))

    def _need(self, eng, waits, sem, val):
        if val <= 0 or (sem == eng and eng == "pe"):
            return
        if self.known[eng].get(sem, 0) >= val:
            return
        waits[sem] = max(waits.get(sem, 0), val)

    def _deps(self, eng, reads, writes, waits=None):
        waits = {} if waits is None else waits
        for r in reads:
            b = self.bufs.get(_key(r))
            if b and b["w"]:
                self._need(eng, waits, *b["w"])
            if b and _key(r)[0] in self.psum_names:
                for s, v in b["r"].items():
                    if s != eng:
                        self._need(eng, waits, s, v)
        for w in writes:
            b = self.bufs.get(_key(w))
            if b:
                if b["w"]:
                    self._need(eng, waits, *b["w"])
                for s, v in b["r"].items():
                    self._need(eng, waits, s, v)
        for s, v in waits.items():
            self.ops[eng].append(("wait", s, v))
            self.known[eng][s] = v

    def _mark(self, reads, writes, sem, val):
        for r in reads:
            b = self.bufs.setdefault(_key(r), {"w": None, "r": {}})
            b["r"][sem] = max(b["r"].get(sem, 0), val)
        for w in writes:
            self.bufs[_key(w)] = {"w": (sem, val), "r": {}}

    def op(self, eng, fn, reads=(), writes=()):
        self._deps(eng, reads, writes)
        self.cnt[eng] += 1
        self.ops[eng].append(("op", fn, eng, 1))
        self._mark(reads, writes, eng, self.cnt[eng])

    def dma(self, q, out, in_, reads=(), writes=()):
        names = self.dma_names[q]
        s = names[self.dma_rr[q] % len(names)]
        self.dma_rr[q] += 1
        waits = {}
        self._need(q, waits, s, self.cnt[s])
        self._deps(q, reads, writes, waits)
        self.cnt[s] += 16
        self.ops[q].append(("op", lambda e: e.dma_start(out=out, in_=in_), s, 16))
        self._mark(reads, writes, s, self.cnt[s])

    def X(self, eng, method, reads, writes, **kw):
        self.op(eng, lambda e: getattr(e, method)(**kw), reads, writes)

    def mm(self, out, lhsT, rhs, start, stop, reads, writes):
        self.op("pe", lambda e: e.matmul(out, lhsT, rhs, start=start, stop=stop), reads, writes)

    def act(self, out, in_, func, reads, writes, bias=None, scale=None):
        kw = {}
        if bias is not None:
            kw["bias"] = bias
        if scale is not None:
            kw["scale"] = scale
        self.op("act", lambda e: e.activation(out=out, in_=in_, func=func, **kw), reads, writes)


class Gemm:
    def __init__(self, P, kc_max, nb, nbuf=2, npsum=4, wdtype=BF16):
        self.P = P
        self.nb = nb
        self.wb = [P.sb("wb", [128, kc_max, nb], wdtype) for _ in range(nbuf)]
        self.pg = [P.ps("pg", [128, 512], F32) for _ in range(npsum)]
        self.wi = 0
        self.pi = 0

    def next_ps(self):
        p = self.pg[self.pi % len(self.pg)]
        self.pi += 1
        return p

    def load_w(self, wv, k0, kc, c0, ncols, q="pool", piece=8):
        P = self.P
        buf = self.wb[self.wi % len(self.wb)]
        self.wi += 1
        for a in range(0, kc, piece):
            b = min(kc, a + piece)
            P.dma(q, buf[:, a:b, 0:ncols], wv[:, k0 + a:k0 + b, c0:c0 + ncols], writes=[buf.k(a)])
        return buf, [buf.k(a) for a in range(0, kc, piece)]

    def run_fm(self, xT, xkeys, kc, ntok, wv, k0, c0, ncols, epi, piece=8):
        P = self.P
        for cb in range(c0, c0 + ncols, self.nb):
            nbc = min(self.nb, c0 + ncols - cb)
            buf, wkeys = self.load_w(wv, k0, kc, cb, nbc, piece=piece)
            for n0 in range(0, nbc, 128):
                n = min(128, nbc - n0)
                pss = [self.next_ps() for _ in range(ntok // 512)]
                for k in range(kc):
                    for tb, ps in enumerate(pss):
                        P.mm(ps[0:n, :], buf[:, k, n0:n0 + n], xT[:, k, tb * 512:(tb + 1) * 512],
                             k == 0, k == kc - 1, reads=[buf.k((k // piece) * piece)] + xkeys, writes=[ps])
                for tb, ps in enumerate(pss):
                    epi(cb + n0, n, tb, ps)

    def run_tm(self, xT, xkeys, kc, ntok, wv, k0, c0, ncols, epi, piece=8):
        P = self.P
        for cb in range(c0, c0 + ncols, self.nb):
            nbc = min(self.nb, c0 + ncols - cb)
            buf, wkeys = self.load_w(wv, k0, kc, cb, nbc, piece=piece)
            for tt in range(ntok // 128):
                ps = self.next_ps()
                for k in range(kc):
                    P.mm(ps[:, 0:nbc], xT[:, k, tt * 128:(tt + 1) * 128], buf[:, k, 0:nbc],
                         k == 0, k == kc - 1, reads=[buf.k((k // piece) * piece)] + xkeys, writes=[ps])
                epi(tt, cb, nbc, ps)


class Ctx:
    pass


def dram(nc, name, shape, dtype, kind="Internal"):
    return nc.dram_tensor(name, list(shape), dtype, kind=kind).ap()


def rms_build(P, C, XT, t0, wn, uT, ones_f, pst=None, rstd_ready=None):
    xk = [P.sb("xk", [128, 1024], F32) for _ in range(3)]
    sq = [P.sb("sq", [128, 1024], F32) for _ in range(2)]
    rstd = P.sb("rstd", [128, 1024], F32)
    if rstd_ready is None:
        own = pst is None
        if own:
            pst = [P.ps("pst", [128, 512], F32) for _ in range(2)]
        for k in range(32):
            x = xk[k % 3]
            s = sq[k % 2]
            P.dma("sp", x[:], XT[k * 128:(k + 1) * 128, t0:t0 + 1024], reads=[("XT", t0 // 1024)], writes=[x])
            P.act(s[:], x[:], AF.Square, reads=[x], writes=[s])
            for tb in range(2):
                P.mm(pst[tb][:], ones_f[:], s[:, tb * 512:(tb + 1) * 512], k == 0, k == 31, reads=[s, ones_f], writes=[pst[tb]])
        for tb in range(2):
            P.act(rstd[:, tb * 512:(tb + 1) * 512], pst[tb][:], AF.Sqrt, reads=[pst[tb], C.epsb], writes=[rstd.k(tb)],
                  bias=C.epsb[:, 0:1], scale=1.0 / D)
            P.X("dve", "reciprocal", [rstd.k(tb)], [rstd.k(tb)], out=rstd[:, tb * 512:(tb + 1) * 512], in_=rstd[:, tb * 512:(tb + 1) * 512])
    else:
        pst = rstd_ready
        for tb in range(2):
            P.act(rstd[:, tb * 512:(tb + 1) * 512], pst[tb][:], AF.Sqrt, reads=[pst[tb], C.epsb], writes=[rstd.k(tb)],
                  bias=C.epsb[:, 0:1], scale=1.0 / D)
            P.X("dve", "reciprocal", [rstd.k(tb)], [rstd.k(tb)], out=rstd[:, tb * 512:(tb + 1) * 512], in_=rstd[:, tb * 512:(tb + 1) * 512])
    for k in range(32):
        x = xk[k % 3]
        P.dma("sp", x[:], XT[k * 128:(k + 1) * 128, t0:t0 + 1024], reads=[("XT", t0 // 1024)], writes=[x])
        P.X("dve", "scalar_tensor_tensor", [x, wn, rstd.k(0), rstd.k(1)], [uT.k(k)],
            out=uT[:, k, :], in0=x[:], scalar=wn[:, k:k + 1], in1=rstd[:], op0=ALU.mult, op1=ALU.mult)
    return rstd


def evac(P, i, out, in_, reads, writes):
    if i % 2 == 0:
        P.op("act", lambda e: e.copy(out=out, in_=in_), reads, writes)
    else:
        P.X("dve", "tensor_copy", reads, writes, out=out, in_=in_)


def phase_A(P, C, l, t0):
    S = C.S
    P.begin()
    ones_f = P.sb("ones_f", [128, 128], F32)
    P.X("dve", "memset", [], [ones_f], ap=ones_f[:], constant=1.0)
    C.epsb = P.sb("epsb", [128, 1], F32)
    P.X("dve", "memset", [], [C.epsb], ap=C.epsb[:], constant=EPS)
    wn = P.sb("wn", [128, 32], F32)
    P.dma("sp", wn[:], C.mix_norm[l], writes=[wn])
    bg = P.sb("bg", [128, 96], F32)
    P.dma("sp", bg[:], C.b_gate[l], writes=[bg])
    uT = P.sb("uT", [128, 32, 1024], BF16)
    ukeys = [uT.k(k) for k in range(32)]
    G = Gemm(P, 32, 512, nbuf=2, npsum=4)
    rms_build(P, C, S.XT, t0, wn, uT, ones_f)
    wv = C.w_in[l].rearrange("(k p) n -> p k n", p=128)
    st = [P.sb("st", [128, 512], F32) for _ in range(4)]
    cnt = [0]

    def tm_epi(dst, dcol0, func=None):
        def epi(tt, col0, n, ps):
            i = cnt[0]
            cnt[0] += 1
            s = st[i % 4]
            if func is None:
                evac(P, i, s[:, 0:n], ps[:, 0:n], [ps], [s])
            else:
                P.act(s[:, 0:n], ps[:, 0:n], func, reads=[ps], writes=[s])
            c = col0 - dcol0
            P.dma("sp", dst[t0 + tt * 128:t0 + (tt + 1) * 128, c:c + n], s[:, 0:n], reads=[s], writes=[("A_out", i)])
        return epi

    def fm_epi(dst, dcol0, func=None, bias_of=None):
        def epi(col0, n, tb, ps):
            i = cnt[0]
            cnt[0] += 1
            s = st[i % 4]
            c = col0 - dcol0
            if func is None:
                evac(P, i, s[0:n, :], ps[0:n, :], [ps], [s])
            elif bias_of is None:
                P.act(s[0:n, :], ps[0:n, :], func, reads=[ps], writes=[s])
            else:
                kk = c // 128
                P.act(s[0:n, :], ps[0:n, :], func, reads=[ps, bias_of], writes=[s], bias=bias_of[0:n, kk:kk + 1])
            P.dma("sp", dst[c:c + n, t0 + tb * 512:t0 + (tb + 1) * 512], s[0:n, :], reads=[s], writes=[("A_out", i)])
        return epi

    G.run_tm(uT, ukeys, 32, 1024, wv, 0, C_HY, 6144, tm_epi(S.HY, C_HY))
    G.run_tm(uT, ukeys, 32, 1024, wv, 0, C_Q, 2048, tm_epi(S.QK, C_Q))
    G.run_tm(uT, ukeys, 32, 1024, wv, 0, C_V, 2048, tm_epi(S.VV, C_V))
    G.run_tm(uT, ukeys, 32, 1024, wv, 0, C_O, 2048, tm_epi(S.OO, C_O, AF.Sigmoid))
    G.run_tm(uT, ukeys, 32, 1024, wv, 0, C_MG, 32, tm_epi(S.MG, C_MG))
    G.run_fm(uT, ukeys, 32, 1024, wv, 0, C_RX, 2048, fm_epi(S.RGX, C_RX))
    G.run_fm(uT, ukeys, 32, 1024, wv, 0, C_RY, 2048, fm_epi(S.RGY, C_RY, AF.Gelu))
    G.run_fm(uT, ukeys, 32, 1024, wv, 0, C_G, 12288, fm_epi(S.GT, C_G, AF.Sigmoid, bg))
    P.end()


def make_ctx(nc, NT, L, kinds=None, consts_kind="ExternalInput"):
    kinds = kinds or {}
    C = Ctx()
    S = Ctx()
    C.S = S
    C.NT = NT

    def sc(name, shape, dt):
        return dram(nc, name, shape, dt, kinds.get(name, "Internal"))

    def inp(name, shape, dt=F32):
        return dram(nc, name, shape, dt, "ExternalInput")

    S.XT = sc("XT", [D, NT], F32)
    S.HY = sc("HY", [NT, 6144], F32)
    S.QK = sc("QK", [NT, 2048], F32)
    S.VV = sc("VV", [NT, 2048], F32)
    S.OO = sc("OO", [NT, 2048], F32)
    S.MG = sc("MG", [NT, 32], F32)
    S.RGX = sc("RGX", [2048, NT], F32)
    S.RGY = sc("RGY", [2048, NT], F32)
    S.GT = sc("GT", [12288, NT], F32)
    S.YT = sc("YT", [6144, NT], BF16)
    S.MT = sc("MT", [D, NT], BF16)
    S.FF = sc("FF", [FFN, NT], BF16)
    C.inp = inp
    C.sc = sc
    C.L = L
    return C


def add_inputs_A(C):
    L = C.L
    C.mix_norm = C.inp("mix_norm", [L, 128, 32])
    C.w_in = C.inp("w_in", [L, D, D_IN])
    C.b_gate = C.inp("b_gate", [L, 128, 96])


def add_inputs_R(C):
    L = C.L
    C.rg_cw = C.inp("rg_cw", [L, 128, 16, 4])
    C.rg_cb = C.inp("rg_cb", [L, 128, 16])
    C.rg_ba = C.inp("rg_ba", [L, 128, 2, 16])
    C.rg_bx = C.inp("rg_bx", [L, 128, 2, 16])
    C.rg_lam = C.inp("rg_lam", [L, 128, 2, 16])
    C.rg_wa = C.inp("rg_wa", [L, 2, 8, 256, 256])
    C.rg_wx = C.inp("rg_wx", [L, 2, 8, 256, 256])


def phase_R(P, C, l, t0):
    S = C.S
    T = SEQ
    P.begin()
    cw = P.sb("cw", [128, 16, 4], F32)
    cb = P.sb("cb", [128, 16], F32)
    ba = P.sb("ba", [128, 2, 16], F32)
    bx = P.sb("bx", [128, 2, 16], F32)
    lam = P.sb("lam", [128, 2, 16], F32)
    c1 = P.sb("c1", [128, 2, 16], F32)
    for t, src in ((cw, C.rg_cw), (cb, C.rg_cb), (ba, C.rg_ba), (bx, C.rg_bx), (lam, C.rg_lam)):
        P.dma("sp", t[:], src[l], writes=[t])
    P.act(c1[:], lam[:], AF.Exp, reads=[lam], writes=[c1], scale=-1.0)
    P.act(c1[:], c1[:], AF.Ln, reads=[c1], writes=[c1], bias=1.0)
    P.X("dve", "tensor_scalar", [c1], [c1], out=c1[:], in0=c1[:], scalar1=-8.0, scalar2=None, op0=ALU.mult)
    wbf = [[P.sb("rgw", [128, 2, 256], BF16) for _ in range(4)] for _ in range(2)]
    xr = [P.sb("xr", [128, T + 3], F32) for _ in range(2)]
    xc = [P.sb("xc", [128, T], F32) for _ in range(2)]
    xcb = [P.sb("xcb", [128, T], BF16) for _ in range(2)]
    r_t = P.sb("r_t", [128, T], F32)
    ig_t = P.sb("ig_t", [128, T], F32)
    a_t = P.sb("a_t", [128, T], F32)
    b_t = P.sb("b_t", [128, T], F32)
    tmp = P.sb("tmp", [128, T], F32)
    hf = P.sb("hf", [128, T], F32)
    hb = P.sb("hb", [128, T], F32)
    yr = P.sb("yr", [128, T], F32)
    yo = [P.sb("yo", [128, T], BF16) for _ in range(2)]
    psa = [P.ps("psa", [128, 512], F32) for _ in range(4)]
    psx = [P.ps("psx", [128, 512], F32) for _ in range(4)]
    for i in range(2):
        P.X("dve", "memset", [], [xr[i]], ap=xr[i][:, 0:2], constant=0.0)
        P.X("dve", "memset", [], [xr[i]], ap=xr[i][:, T + 2:T + 3], constant=0.0)
    for h in range(NHEAD):
        ws = wbf[h % 2]
        for d in range(2):
            P.dma("pool", ws[d][:], C.rg_wa[l, d, h].rearrange("(it p) j -> p it j", p=128), writes=[ws[d]])
            P.dma("pool", ws[2 + d][:], C.rg_wx[l, d, h].rearrange("(it p) j -> p it j", p=128), writes=[ws[2 + d]])
        for it in range(2):
            ct = h * 2 + it
            P.dma("sp", xr[it][:, 2:T + 2], S.RGX[ct * 128:(ct + 1) * 128, t0:t0 + T], writes=[xr[it]])
            P.X("dve", "tensor_scalar", [xr[it], cw, cb], [xc[it]], out=xc[it][:], in0=xr[it][:, 0:T],
                scalar1=cw[:, ct, 0:1], scalar2=cb[:, ct:ct + 1], op0=ALU.mult, op1=ALU.add)
            for k in range(1, 4):
                P.X("dve", "scalar_tensor_tensor", [xr[it], cw, xc[it]], [xc[it]], out=xc[it][:], in0=xr[it][:, k:k + T],
                    scalar=cw[:, ct, k:k + 1], in1=xc[it][:], op0=ALU.mult, op1=ALU.add)
            P.op("act", lambda e, it=it: e.copy(out=xcb[it][:], in_=xc[it][:]), [xc[it]], [xcb[it]])
        for jt in range(2):
            ct = h * 2 + jt
            for d in range(2):
                for tb in range(4):
                    for it in range(2):
                        P.mm(psa[tb][:], ws[d][:, it, jt * 128:(jt + 1) * 128], xcb[it][:, tb * 512:(tb + 1) * 512],
                             it == 0, it == 1, reads=[ws[d], xcb[it]], writes=[psa[tb]])
                    for it in range(2):
                        P.mm(psx[tb][:], ws[2 + d][:, it, jt * 128:(jt + 1) * 128], xcb[it][:, tb * 512:(tb + 1) * 512],
                             it == 0, it == 1, reads=[ws[2 + d], xcb[it]], writes=[psx[tb]])
                for tb in range(4):
                    sl = slice(tb * 512, (tb + 1) * 512)
                    P.act(r_t[:, sl], psa[tb][:], AF.Sigmoid, reads=[psa[tb], ba], writes=[r_t], bias=ba[:, d, ct:ct + 1])
                    P.act(ig_t[:, sl], psx[tb][:], AF.Sigmoid, reads=[psx[tb], bx], writes=[ig_t], bias=bx[:, d, ct:ct + 1])
                P.act(a_t[:], r_t[:], AF.Exp, reads=[r_t, c1], writes=[a_t], scale=c1[:, d, ct:ct + 1])
                P.X("pool", "tensor_tensor", [a_t], [tmp], out=tmp[:], in0=a_t[:], in1=a_t[:], op=ALU.mult)
                P.act(tmp[:], tmp[:], AF.Sqrt, reads=[tmp], writes=[tmp], bias=1.0, scale=-1.0)
                P.X("pool", "tensor_tensor", [tmp, ig_t], [b_t], out=b_t[:], in0=tmp[:], in1=ig_t[:], op=ALU.mult)
                P.X("dve", "tensor_tensor", [b_t, xc[jt]], [b_t], out=b_t[:], in0=b_t[:], in1=xc[jt][:], op=ALU.mult)
                if d == 0:
                    P.X("dve", "tensor_tensor_scan", [a_t, b_t], [hf], out=hf[:], data0=a_t[:], data1=b_t[:],
                        initial=0.0, op0=ALU.mult, op1=ALU.add)
                else:
                    P.X("dve", "tensor_tensor_scan", [a_t, b_t], [hb], out=hb[:, ::-1], data0=a_t[:, ::-1], data1=b_t[:, ::-1],
                        initial=0.0, op0=ALU.mult, op1=ALU.add)
            P.dma("sp", yr[:], S.RGY[ct * 128:(ct + 1) * 128, t0:t0 + T], writes=[yr])
            P.X("pool", "tensor_tensor", [hf, hb], [hf], out=hf[:], in0=hf[:], in1=hb[:], op=ALU.add)
            o = yo[jt]
            P.X("dve", "tensor_tensor", [hf, yr], [o], out=o[:], in0=hf[:], in1=yr[:], op=ALU.mult)
            P.dma("sp", S.YT[4096 + ct * 128:4096 + (ct + 1) * 128, t0:t0 + T], o[:], reads=[o], writes=[("YTc", ct)])
    P.end()


def add_inputs_M(C):
    L = C.L
    if not hasattr(C, "c_ident"):
        C.c_ident = C.inp("c_ident", [128, 128])
    C.c_tri = C.inp("c_tri", [64, 2, 64])
    C.ml_gb = C.inp("ml_gb", [L, 1, 32])
    C.ml_norm = C.inp("ml_norm", [L, 1, 2048])


def phase_M(P, C, l, t0):
    S = C.S
    NCH = 32
    P.begin()
    ident_f = P.sb("ident_f", [128, 128], F32)
    ident_b = P.sb("ident_b", [128, 128], BF16)
    tri = P.sb("tri", [64, 2, 64], F32)
    ones64 = P.sb("ones64", [64, 128], F32)
    epsb = P.sb("epsb", [128, 1], F32)
    P.dma("sp", ident_f[:], C.c_ident, writes=[ident_f])
    P.dma("sp", tri[:], C.c_tri, writes=[tri])
    P.X("dve", "tensor_copy", [ident_f], [ident_b], out=ident_b[:], in_=ident_f[:])
    P.X("dve", "memset", [], [ones64], ap=ones64[:], constant=1.0)
    P.X("dve", "memset", [], [epsb], ap=epsb[:], constant=EPS)
    mgraw = P.sb("mgraw", [64, NCH, 32], F32)
    gb = P.sb("gb", [64, 32], F32)
    mln = P.sb("mln", [64, 2048], F32)
    P.dma("sp", mgraw[:], S.MG[t0:t0 + SEQ, :].rearrange("(c j) k -> j c k", j=64), writes=[mgraw])
    P.dma("sp", gb[:], C.ml_gb[l].to_broadcast([64, 32]), writes=[gb])
    P.dma("sp", mln[:], C.ml_norm[l].to_broadcast([64, 2048]), writes=[mln])
    G = P.sb("G", [64, 32, NCH], F32)
    P.X("dve", "tensor_tensor", [mgraw, gb], [G], out=G[:], in0=mgraw[:].rearrange("p c k -> p k c"),
        in1=gb[:].unsqueeze(2).to_broadcast([64, 32, NCH]), op=ALU.add)
    Gv = G[:].rearrange("p (d g h) c -> p d g h c", d=2, g=2)
    spt = P.sb("spt", [64, 2, 8, NCH], F32)
    P.act(spt[:], Gv[:, :, 1], AF.Exp, reads=[G], writes=[spt], scale=-1.0)
    P.act(spt[:], spt[:], AF.Ln, reads=[spt], writes=[spt], bias=1.0)
    banks = [P.ps("bank", [128, 512], F32) for _ in range(6)]
    tps = [P.ps("tps", [128, 512], BF16) for _ in range(2)]
    eb = P.sb("eb", [64, 2, 256], F32)
    imb = P.sb("imb", [64, 2, 256], F32)
    eib = P.sb("eib", [64, 2, 256], F32)
    ekk = P.sb("ekk", [64, 2, 256], F32)
    eg = P.sb("eg", [128, 2, 256], F32)
    for d in range(2):
        pb = banks[d]
        pg = banks[2 + d]
        rhs = spt[:, d].rearrange("p h c -> p (h c)")
        P.mm(pb[0:64, 0:256], tri[:, d, :], rhs, True, True, reads=[tri, spt], writes=[pb])
        P.mm(pg[:, 0:256], ones64[:], rhs, True, True, reads=[ones64, spt], writes=[pg])
        P.act(eb[:, d, :], pb[0:64, 0:256], AF.Exp, reads=[pb], writes=[eb], scale=-1.0)
        P.X("dve", "tensor_tensor", [G, pb, eb], [imb], out=imb[:, d, :].rearrange("p (h c) -> p h c", h=8),
            in0=Gv[:, d, 0], in1=pb[0:64, 0:256].rearrange("p (h c) -> p h c", h=8), op=ALU.add)
        P.act(eib[:, d, :], imb[:, d, :], AF.Exp, reads=[imb], writes=[eib])
        P.X("dve", "tensor_tensor", [imb, pg], [ekk], out=ekk[:, d, :], in0=imb[:, d, :], in1=pg[0:64, 0:256], op=ALU.subtract)
        P.act(ekk[:, d, :], ekk[:, d, :], AF.Exp, reads=[ekk], writes=[ekk])
        P.act(eg[:, d, :], pg[:, 0:256], AF.Exp, reads=[pg], writes=[eg], scale=-1.0)
    qraw = P.sb("qraw", [64, NCH, 128], F32)
    kraw = P.sb("kraw", [64, NCH, 128], F32)
    vaug = P.sb("vaug", [64, NCH, 260], BF16)
    qs1 = P.sb("qs", [64, NCH, 128], BF16)
    ks1 = P.sb("ks", [64, NCH, 128], BF16)
    qs = [qs1, qs1]
    ks = [ks1, ks1]
    kk = [P.sb("kk", [64, NCH, 128], BF16) for _ in range(2)]
    qT = [P.sb("qT", [128, SEQ], BF16) for _ in range(2)]
    kT = [P.sb("kT", [128, SEQ], BF16) for _ in range(2)]
    CTf = [P.sb("CTf", [128, 257], F32) for _ in range(2)]
    CTall = P.sb("CTall", [128, NCH, 258], BF16)
    STs = [P.sb("STs", [64, 64], BF16) for _ in range(4)]
    HS = P.sb("HS", [64, NCH, 256], F32)
    dn = [P.sb("dn", [64, 1], F32) for _ in range(4)]
    sq = P.sb("sq", [64, 8, 256], F32)
    so = P.sb("so", [64, 8, 256], F32)
    ss = P.sb("ss", [64, 8], F32)
    ytm = P.sb("ytm", [64, 8, 256], BF16)
    yTb = [P.sb("yTb", [128, 512], BF16) for _ in range(2)]
    P.X("dve", "memset", [], [vaug.k("one")], ap=vaug[:, :, 256:257], constant=1.0)
    sc = float(DK) ** -0.5
    for h in range(NHEAD):
        tok = S.QK[t0:t0 + SEQ, :].rearrange("(c j) f -> j c f", j=64)
        P.dma("sp", qraw[:], tok[:, :, h * 128:(h + 1) * 128], writes=[qraw])
        P.dma("sp", kraw[:], tok[:, :, 1024 + h * 128:1024 + (h + 1) * 128], writes=[kraw])
        P.dma("pool", vaug[:, :, 0:256], S.VV[t0:t0 + SEQ, :].rearrange("(c j) f -> j c f", j=64)[:, :, h * 256:(h + 1) * 256],
              writes=[vaug])
        for d in range(2):
            hs = slice(h * NCH, (h + 1) * NCH)
            bc = lambda t: t[:, d, hs].unsqueeze(2).to_broadcast([64, NCH, 128])
            P.X("dve", "tensor_tensor", [qraw, eb], [qs[d]], out=qs[d][:], in0=qraw[:], in1=bc(eb), op=ALU.mult)
            P.X("dve", "scalar_tensor_tensor", [kraw, eib], [ks[d]], out=ks[d][:], in0=kraw[:], scalar=sc, in1=bc(eib),
                op0=ALU.mult, op1=ALU.mult)
            P.X("dve", "scalar_tensor_tensor", [kraw, ekk], [kk[d]], out=kk[d][:], in0=kraw[:], scalar=sc, in1=bc(ekk),
                op0=ALU.mult, op1=ALU.mult)
            ti = 0
            for src, dst in ((qs[d], qT[d]), (ks[d], kT[d])):
                for c8 in range(4):
                    tp = tps[ti % 2]
                    ti += 1
                    for cc in range(8):
                        c = c8 * 8 + cc
                        P.op("pe", lambda e, tp=tp, src=src, c=c, cc=cc: e.transpose(tp[:, cc * 64:(cc + 1) * 64], src[:, c, :], ident_b[0:64, 0:64]),
                             [src, ident_b], [tp])
                    evac(P, ti, dst[:, c8 * 512:(c8 + 1) * 512], tp[:], [tp], [dst.k(c8)])
        for d in range(2):
            for s in range(NCH - 1):
                c = s if d == 0 else NCH - 1 - s
                dps = banks[4 + (s % 2)]
                cur = CTf[s % 2]
                prev = CTf[(s + 1) % 2]
                P.mm(dps[:, 0:257], kk[d][:, c, :], vaug[:, c, 0:257], True, True, reads=[kk[d], vaug, vaug.k("one")], writes=[dps])
                if s == 0:
                    P.X("dve", "tensor_copy", [dps], [cur], out=cur[:], in_=dps[:, 0:257])
                else:
                    col = h * NCH + c
                    P.X("dve", "scalar_tensor_tensor", [prev, eg, dps], [cur], out=cur[:], in0=prev[:],
                        scalar=eg[:, d, col:col + 1], in1=dps[:, 0:257], op0=ALU.mult, op1=ALU.add)
                P.op("act", lambda e, cur=cur, s=s: e.copy(out=CTall[:, s, 0:257], in_=cur[:]), [cur], [CTall.k(s)])
            for s in range(NCH):
                c = s if d == 0 else NCH - 1 - s
                cs = slice(c * 64, (c + 1) * 64)
                c8 = c // 8
                stp = banks[s % 2]
                ops_ = banks[2 + (s % 2)]
                sts = STs[s % 4]
                P.mm(stp[0:64, 0:64], kT[d][:, cs], qT[d][:, cs], True, True, reads=[kT[d].k(c8), qT[d].k(c8)], writes=[stp])
                P.X("dve", "tensor_tensor", [stp, tri], [sts], out=sts[:], in0=stp[0:64, 0:64], in1=tri[:, d, :], op=ALU.mult)
                if s > 0:
                    P.mm(ops_[0:64, 0:257], qT[d][:, cs], CTall[:, s - 1, 0:257], True, False, reads=[qT[d].k(c8), CTall.k(s - 1)], writes=[ops_])
                P.mm(ops_[0:64, 0:257], sts[:], vaug[:, c, 0:257], s == 0, True, reads=[sts, vaug, vaug.k("one")], writes=[ops_])
                dd = dn[s % 4]
                P.op("act", lambda e, dd=dd, ops_=ops_: e.activation(out=dd[:], in_=ops_[0:64, 256:257], func=AF.Abs), [ops_], [dd])
                P.X("dve", "tensor_scalar", [dd], [dd], out=dd[:], in0=dd[:], scalar1=1.0, scalar2=None, op0=ALU.max)
                P.X("dve", "reciprocal", [dd], [dd], out=dd[:], in_=dd[:])
                if d == 0:
                    P.X("dve", "tensor_scalar", [ops_, dd], [HS.k(c)], out=HS[:, c, :], in0=ops_[0:64, 0:256],
                        scalar1=dd[:, 0:1], scalar2=None, op0=ALU.mult)
                else:
                    P.X("dve", "scalar_tensor_tensor", [ops_, dd, HS.k(c)], [HS.k(c)], out=HS[:, c, :], in0=ops_[0:64, 0:256],
                        scalar=dd[:, 0:1], in1=HS[:, c, :], op0=ALU.mult, op1=ALU.add)
        for cg in range(4):
            cr = slice(cg * 8, (cg + 1) * 8)
            hk = [HS.k(c) for c in range(cg * 8, cg * 8 + 8)]
            P.dma("sp", so[:], S.OO[t0 + cg * 512:t0 + (cg + 1) * 512, h * 256:(h + 1) * 256].rearrange("(c j) f -> j c f", j=64), writes=[so])
            P.X("pool", "tensor_tensor", hk, [sq], out=sq[:], in0=HS[:, cr, :], in1=HS[:, cr, :], op=ALU.mult)
            P.X("dve", "tensor_reduce", [sq], [ss], out=ss[:], in_=sq[:], axis=AX.X, op=ALU.add)
            P.act(ss[:], ss[:], AF.Sqrt, reads=[ss, epsb], writes=[ss], bias=epsb[0:64, 0:1], scale=1.0 / DV)
            P.X("dve", "reciprocal", [ss], [ss], out=ss[:], in_=ss[:])
            P.X("dve", "tensor_tensor", hk + [ss], [sq], out=sq[:], in0=HS[:, cr, :], in1=ss[:].unsqueeze(2).to_broadcast([64, 8, 256]), op=ALU.mult)
            P.X("pool", "tensor_tensor", [sq, mln], [sq], out=sq[:], in0=sq[:],
                in1=mln[:, h * 256:(h + 1) * 256].unsqueeze(1).to_broadcast([64, 8, 256]), op=ALU.mult)
            P.X("dve", "tensor_tensor", [sq, so], [ytm], out=ytm[:], in0=sq[:], in1=so[:], op=ALU.mult)
            for vt in range(2):
                tp = tps[vt]
                for cc in range(8):
                    P.op("pe", lambda e, tp=tp, cc=cc, vt=vt: e.transpose(tp[:, cc * 64:(cc + 1) * 64], ytm[:, cc, vt * 128:(vt + 1) * 128], ident_b[0:64, 0:64]),
                         [ytm, ident_b], [tp])
                o = yTb[vt]
                evac(P, vt, o[:], tp[:], [tp], [o])
                r0 = 2048 + h * 256 + vt * 128
                P.dma(STQ, S.YT[r0:r0 + 128, t0 + cg * 512:t0 + (cg + 1) * 512], o[:], reads=[o], writes=[("YTb", h, cg, vt)])
    P.end()


NFFT = 2 * SEQ
CSQ = "act"
STQ = "pool"
PE2 = "dve"
NOCS = False


def add_inputs_H(C):
    L = C.L
    if not hasattr(C, "c_ident"):
        C.c_ident = C.inp("c_ident", [128, 128])
    C.c_featT = C.inp("c_featT", [33, SEQ])
    C.c_win = C.inp("c_win", [SEQ, 2048])
    C.c_cm = C.inp("c_cm", [16, 128, 16, 128], BF16)
    C.c_sm = C.inp("c_sm", [16, 128, 16, 128], BF16)
    C.c_alt = C.inp("c_alt", [128, 128], BF16)
    C.c_nyq = C.inp("c_nyq", [128, 128], BF16)
    C.hy_w1 = C.inp("hy_w1", [L, 33, 64])
    C.hy_b1 = C.inp("hy_b1", [L, 64, 1])
    C.hy_w2 = C.inp("hy_w2", [L, 2, 64, 64])
    C.hy_b2 = C.inp("hy_b2", [L, 64, 2])
    C.hy_freq = C.inp("hy_freq", [L, 64, 1])
    C.hy_w3 = C.inp("hy_w3", [L, 64, 8192])
    C.hy_cw = C.inp("hy_cw", [L, 3, 6144])
    C.hy_cb = C.inp("hy_cb", [L, 1, 6144])
    C.hy_skip = C.inp("hy_skip", [L, 2, 2048])
    S = C.S
    S.PF = C.sc("PF", [2, 2, 17 * 128, 2048], F32)
    S.VC = C.sc("VC", [2, SEQ, 512], F32)
    S.ZC = C.sc("ZC", [SEQ, 512], F32)


def phase_HF(P, C, l):
    S = C.S
    P.begin()
    featT = P.sb("featT", [33, SEQ], F32)
    w1 = P.sb("w1", [33, 64], F32)
    w2 = P.sb("w2", [64, 2, 64], F32)
    b1 = P.sb("b1", [64, 1], F32)
    b2 = P.sb("b2", [64, 2], F32)
    fq = P.sb("fq", [64, 1], F32)
    fb = P.sb("fb", [64, 3], F32)
    w3 = P.sb("w3", [64, 8192], F32)
    ones_f = P.sb("ones_f", [128, 128], F32)
    altT = P.sb("altT", [128, 128], BF16)
    P.dma("sp", featT[:], C.c_featT, writes=[featT])
    P.dma("sp", w1[:], C.hy_w1[l], writes=[w1])
    P.dma("sp", w2[:], C.hy_w2[l].rearrange("j i o -> i j o"), writes=[w2])
    P.dma("sp", b1[:], C.hy_b1[l], writes=[b1])
    P.dma("sp", b2[:], C.hy_b2[l], writes=[b2])
    P.dma("sp", fq[:], C.hy_freq[l], writes=[fq])
    P.dma("sp", w3[:], C.hy_w3[l], writes=[w3])
    P.dma("sp", altT[:], C.c_alt, writes=[altT])
    P.X("dve", "memset", [], [ones_f], ap=ones_f[:], constant=1.0)
    P.X("dve", "tensor_scalar", [b1, fq], [fb], out=fb[:, 0:1], in0=b1[:], scalar1=fq[:, 0:1], scalar2=None, op0=ALU.mult)
    P.X("dve", "tensor_scalar", [b2, fq, fb], [fb], out=fb[:, 1:3], in0=b2[:], scalar1=fq[:, 0:1], scalar2=None, op0=ALU.mult)
    hT = [P.sb("hT", [64, SEQ], F32) for _ in range(2)]
    rr = P.sb("rr", [64, 512], F32)
    r2 = P.sb("r2", [64, 512], F32)
    pm = [P.ps("pm", [128, 512], F32) for _ in range(2)]
    pf = [P.ps("pf", [128, 512], F32) for _ in range(2)]
    pbk = [P.ps("pbk", [128, 512], F32) for _ in range(2)]
    psn = P.ps("psn", [128, 512], F32)
    pnq = P.ps("pnq", [128, 512], F32)
    for layer in range(3):
        src = featT if layer == 0 else hT[(layer - 1) % 2]
        dst = hT[layer % 2]
        for tb in range(4):
            ps = pm[tb % 2]
            sl = slice(tb * 512, (tb + 1) * 512)
            if layer == 0:
                P.mm(ps[0:64, :], w1[:], featT[:, sl], True, True, reads=[w1, featT], writes=[ps])
            else:
                P.mm(ps[0:64, :], w2[:, layer - 1, :], src[:, sl], True, True, reads=[w2, src], writes=[ps])
            P.X("dve", "tensor_scalar", [ps, fq, fb], [rr], out=rr[:], in0=ps[0:64, :], scalar1=fq[:, 0:1],
                scalar2=fb[:, layer:layer + 1], op0=ALU.mult, op1=ALU.add)
            P.X("dve", "tensor_scalar", [rr], [r2], out=r2[:], in0=rr[:], scalar1=1.0 / TWO_PI, scalar2=MAGIC, op0=ALU.mult, op1=ALU.add)
            P.X("dve", "tensor_scalar", [r2], [r2], out=r2[:], in0=r2[:], scalar1=MAGIC, scalar2=TWO_PI, op0=ALU.subtract, op1=ALU.mult)
            P.X("dve", "tensor_tensor", [rr, r2], [rr], out=rr[:], in0=rr[:], in1=r2[:], op=ALU.subtract)
            P.X("dve", "tensor_scalar", [rr], [rr], out=rr[:], in0=rr[:], scalar1=float(np.pi), scalar2=-float(np.pi), op0=ALU.min, op1=ALU.max)
            P.act(dst[:, sl], rr[:], AF.Sin, reads=[rr], writes=[dst])
    h3 = P.sb("h3b", [64, SEQ], BF16)
    w3b = P.sb("w3b", [64, 8192], BF16)
    P.X("dve", "tensor_copy", [hT[0]], [h3], out=h3[:], in_=hT[0][:])
    P.X("pool", "tensor_copy", [w3], [w3b], out=w3b[:], in_=w3[:])
    nacc = P.sb("nacc", [128, 512], F32)
    sumt = P.sb("sumt", [128, 16, 512], BF16)
    dift = P.sb("dift", [128, 16, 512], BF16)
    win = [P.sb("win", [128, 512], F32) for _ in range(2)]
    fw = [P.sb("fw", [128, 512], F32) for _ in range(2)]
    bw = [P.sb("bw", [128, 512], F32) for _ in range(2)]
    s1 = [P.sb("s1", [128, 512], F32) for _ in range(2)]
    s2 = [P.sb("s2", [128, 512], F32) for _ in range(2)]
    rs2 = P.sb("rs2", [128, 512], F32)
    rsN = P.sb("rsN", [128, 512], F32)
    cmb = [P.sb("cmb", [128, 16, 128], BF16) for _ in range(3)]
    smb = [P.sb("smb", [128, 16, 128], BF16) for _ in range(3)]
    po = [P.sb("po", [128, 512], F32) for _ in range(4)]
    n = 0
    for o in range(2):
        for cb in range(4):
            cf = o * 4096 + cb * 512
            for i in range(16):
                b = i % 2
                ts_ = slice(i * 128, (i + 1) * 128)
                P.dma("sp", win[b][:], C.c_win[ts_, cb * 512:(cb + 1) * 512], writes=[win[b]])
                P.mm(pf[b][:], h3[:, ts_], w3b[:, cf:cf + 512], True, True, reads=[h3, w3b], writes=[pf[b]])
                P.mm(pbk[b][:], h3[:, ts_], w3b[:, cf + 2048:cf + 2560], True, True, reads=[h3, w3b], writes=[pbk[b]])
                P.X("dve", "tensor_tensor", [pf[b], win[b]], [fw[b]], out=fw[b][:], in0=pf[b][:], in1=win[b][:], op=ALU.mult)
                P.X("dve", "tensor_tensor", [pbk[b], win[b]], [bw[b]], out=bw[b][:], in0=pbk[b][:], in1=win[b][:], op=ALU.mult)
                if i == 0:
                    P.X("dve", "memset", [], [bw[b]], ap=bw[b][0:1, :], constant=0.0)
                P.X("pool", "tensor_tensor", [fw[b], bw[b]], [sumt.k(i)], out=sumt[:, i, :], in0=fw[b][:], in1=bw[b][:], op=ALU.add)
                P.X("pool", "tensor_tensor", [fw[b], bw[b]], [dift.k(i)], out=dift[:, i, :], in0=bw[b][:], in1=fw[b][:], op=ALU.subtract)
                P.act(s1[b][:], fw[b][:], AF.Square, reads=[fw[b]], writes=[s1[b]])
                P.act(s2[b][:], bw[b][:], AF.Square, reads=[bw[b]], writes=[s2[b]])
                if i == 0:
                    P.X("dve", "tensor_tensor", [s1[b], s2[b]], [nacc], out=nacc[:], in0=s1[b][:], in1=s2[b][:], op=ALU.add)
                else:
                    P.X("dve", "tensor_tensor", [s1[b], nacc], [nacc], out=nacc[:], in0=s1[b][:], in1=nacc[:], op=ALU.add)
                    P.X("dve", "tensor_tensor", [s2[b], nacc], [nacc], out=nacc[:], in0=s2[b][:], in1=nacc[:], op=ALU.add)
            P.mm(psn[:], ones_f[:], nacc[:], True, True, reads=[ones_f, nacc], writes=[psn])
            P.act(rs2[:], psn[:], AF.Sqrt, reads=[psn], writes=[rs2])
            P.X("dve", "reciprocal", [rs2], [rs2], out=rs2[:], in_=rs2[:])
            P.X("dve", "tensor_scalar", [rs2], [rsN], out=rsN[:], in0=rs2[:], scalar1=1.0 / NFFT, scalar2=None, op0=ALU.mult)
            P.X("dve", "tensor_scalar", [rs2], [rs2], out=rs2[:], in0=rs2[:], scalar1=2.0 / NFFT, scalar2=None, op0=ALU.mult)
            skeys = [sumt.k(i) for i in range(16)]
            dkeys = [dift.k(i) for i in range(16)]
            for j in range(16):
                cmj = cmb[j % 3]
                smj = smb[j % 3]
                P.dma(CSQ, cmj[:], C.c_cm[j], writes=[cmj])
                P.dma(CSQ, smj[:], C.c_sm[j], writes=[smj])
                pP = pf[j % 2]
                pQ = pbk[j % 2]
                for k in range(16):
                    P.mm(pP[:], cmj[:, k, :], sumt[:, k, :], k == 0, k == 15, reads=[cmj, sumt.k(k)], writes=[pP])
                for k in range(16):
                    P.mm(pQ[:], smj[:, k, :], dift[:, k, :], k == 0, k == 15, reads=[smj, dift.k(k)], writes=[pQ])
                oP = po[n % 4]
                oQ = po[(n + 1) % 4]
                n += 2
                P.X("dve", "tensor_tensor", [pP, rs2], [oP], out=oP[:], in0=pP[:], in1=rs2[:], op=ALU.mult)
                if j == 0:
                    P.X("dve", "tensor_scalar", [oP], [oP], out=oP[0:1, :], in0=oP[0:1, :], scalar1=0.5, scalar2=None, op0=ALU.mult)
                P.X("dve", "tensor_tensor", [pQ, rs2], [oQ], out=oQ[:], in0=pQ[:], in1=rs2[:], op=ALU.mult)
                P.dma(STQ, S.PF[o, 0, j * 128:(j + 1) * 128, cb * 512:(cb + 1) * 512], oP[:], reads=[oP], writes=[("PF", o, 0, j, cb)])
                P.dma(STQ, S.PF[o, 1, j * 128:(j + 1) * 128, cb * 512:(cb + 1) * 512], oQ[:], reads=[oQ], writes=[("PF", o, 1, j, cb)])
            for k in range(16):
                P.mm(pnq[:], altT[:], sumt[:, k, :], k == 0, k == 15, reads=[altT, sumt.k(k)], writes=[pnq])
            oN = po[n % 4]
            n += 1
            P.X("dve", "tensor_tensor", [pnq, rsN], [oN], out=oN[:], in0=pnq[:], in1=rsN[:], op=ALU.mult)
            P.dma(STQ, S.PF[o, 0, 2048:2176, cb * 512:(cb + 1) * 512], oN[:], reads=[oN], writes=[("PF", o, 0, 16, cb)])
    P.end()


def phase_HC(P, C, l, t0):
    S = C.S
    P.begin()
    ident_f = P.sb("ident_f", [128, 128], F32)
    ident_b = P.sb("ident_b", [128, 128], BF16)
    altT = P.sb("altT", [128, 128], BF16)
    nyqT = P.sb("nyqT", [128, 128], BF16)
    P.dma("sp", ident_f[:], C.c_ident, writes=[ident_f])
    P.X("dve", "tensor_copy", [ident_f], [ident_b], out=ident_b[:], in_=ident_f[:])
    P.dma("sp", altT[:], C.c_alt, writes=[altT])
    P.dma("sp", nyqT[:], C.c_nyq, writes=[nyqT])
    vz = P.sb("vz", [128, 16, 512], BF16)
    zz = P.sb("zz", [128, 16, 512], BF16)
    YR = P.sb("YR", [128, 16, 512], BF16)
    YW = P.sb("YW", [128, 16, 512], BF16)
    YN = P.sb("YN", [128, 512], BF16)
    cmb = [P.sb("cmb", [128, 16, 128], BF16) for _ in range(3)]
    smb = [P.sb("smb", [128, 16, 128], BF16) for _ in range(3)]
    pq = [P.sb("pq", [128, 2, 512], F32) for _ in range(2)]
    cwt = [P.sb("cwt", [128, 3, 512], F32) for _ in range(3)]
    cbt = [P.sb("cbt", [128, 512], F32) for _ in range(3)]
    skt = [P.sb("skt", [128, 512], F32) for _ in range(2)]
    xs = [[P.sb("xs", [128, 512], F32) for _ in range(3)] for _ in range(3)]
    ca = [P.sb("ca", [128, 512], F32) for _ in range(2)]
    ct_ = [P.sb("ct", [128, 512], F32) for _ in range(2)]
    tt = [P.sb("tt", [128, 512], F32) for _ in range(4)]
    e1 = [P.sb("e1", [128, 512], F32) for _ in range(2)]
    e2 = [P.sb("e2", [128, 512], F32) for _ in range(2)]
    yb = [P.sb("yb", [128, 512], BF16) for _ in range(2)]
    yT4 = [P.sb("yT4", [128, 4, 128], BF16) for _ in range(2)]
    pA = [P.ps("pA", [128, 512], F32) for _ in range(2)]
    pB = [P.ps("pB", [128, 512], F32) for _ in range(2)]
    pY = [P.ps("pY", [128, 512], F32) for _ in range(2)]
    pN = P.ps("pN", [128, 512], F32)
    pT = P.ps("pT", [128, 512], BF16)
    cnt = {"x": 0, "c": 0, "cs": 0}

    def short_conv(part, cb, i, out_t):
        st = xs[cnt["x"] % 3]
        cnt["x"] += 1
        c0 = part * 2048 + cb * 512
        r0 = t0 + i * 128
        if i == 0:
            P.X("dve", "memset", [], [st[0]], ap=st[0][0:32, :], constant=0.0)
            P.dma("sp", st[0][1:128, :], S.HY[r0:r0 + 127, c0:c0 + 512], writes=[st[0]])
        else:
            P.dma("sp", st[0][:], S.HY[r0 - 1:r0 + 127, c0:c0 + 512], writes=[st[0]])
        P.dma("sp", st[1][:], S.HY[r0:r0 + 128, c0:c0 + 512], writes=[st[1]])
        if i == 15:
            P.X("dve", "memset", [], [st[2]], ap=st[2][96:128, :], constant=0.0)
            P.dma("sp", st[2][0:127, :], S.HY[r0 + 1:r0 + 128, c0:c0 + 512], writes=[st[2]])
        else:
            P.dma("sp", st[2][:], S.HY[r0 + 1:r0 + 129, c0:c0 + 512], writes=[st[2]])
        w = cwt[part]
        a = ca[cnt["c"] % 2]
        t = ct_[cnt["c"] % 2]
        cnt["c"] += 1
        P.X("dve", "tensor_tensor", [st[0], w], [a], out=a[:], in0=st[0][:], in1=w[:, 0, :], op=ALU.mult)
        P.X(PE2, "tensor_tensor", [st[1], w], [t], out=t[:], in0=st[1][:], in1=w[:, 1, :], op=ALU.mult)
        P.X("dve", "tensor_tensor", [a, t], [a], out=a[:], in0=a[:], in1=t[:], op=ALU.add)
        P.X(PE2, "tensor_tensor", [st[2], w], [t], out=t[:], in0=st[2][:], in1=w[:, 2, :], op=ALU.mult)
        P.X(PE2, "tensor_tensor", [a, cbt[part]], [a], out=a[:], in0=a[:], in1=cbt[part][:], op=ALU.add)
        P.X("dve", "tensor_tensor", [a, t], [out_t], out=out_t[:], in0=a[:], in1=t[:], op=ALU.add)

    def load_cs(j):
        b = cnt["cs"] % 3
        cnt["cs"] += 1
        if NOCS and cnt["cs"] > 3:
            return cmb[b], smb[b]
        P.dma(CSQ, cmb[b][:], C.c_cm[j], writes=[cmb[b]])
        P.dma(CSQ, smb[b][:], C.c_sm[j], writes=[smb[b]])
        return cmb[b], smb[b]

    def forward(o, cb, zin):
        zkeys = [zin.k(k) for k in range(16)]
        for j in range(16):
            cmj, smj = load_cs(j)
            f = pq[j % 2]
            P.dma("sp", f[:, 0, :], S.PF[o, 0, j * 128:(j + 1) * 128, cb * 512:(cb + 1) * 512], writes=[f.k(0)])
            P.dma("sp", f[:, 1, :], S.PF[o, 1, j * 128:(j + 1) * 128, cb * 512:(cb + 1) * 512], writes=[f.k(1)])
            a = pA[j % 2]
            b = pB[j % 2]
            for k in range(16):
                P.mm(a[:], cmj[:, k, :], zin[:, k, :], k == 0, k == 15, reads=[cmj, zin.k(k)], writes=[a])
            for k in range(16):
                P.mm(b[:], smj[:, k, :], zin[:, k, :], k == 0, k == 15, reads=[smj, zin.k(k)], writes=[b])
            fk = [f.k(0), f.k(1)]
            P.X("dve", "tensor_tensor", [a] + fk, [tt[0]], out=tt[0][:], in0=a[:], in1=f[:, 0, :], op=ALU.mult)
            P.X("dve", "tensor_tensor", [b] + fk, [tt[1]], out=tt[1][:], in0=b[:], in1=f[:, 1, :], op=ALU.mult)
            P.X("dve", "tensor_tensor", [b] + fk, [tt[2]], out=tt[2][:], in0=b[:], in1=f[:, 0, :], op=ALU.mult)
            P.X("dve", "tensor_tensor", [a] + fk, [tt[3]], out=tt[3][:], in0=a[:], in1=f[:, 1, :], op=ALU.mult)
            P.X(PE2, "tensor_tensor", [tt[0], tt[1]], [YR.k(j)], out=YR[:, j, :], in0=tt[0][:], in1=tt[1][:], op=ALU.add)
            P.X(PE2, "tensor_tensor", [tt[2], tt[3]], [YW.k(j)], out=YW[:, j, :], in0=tt[2][:], in1=tt[3][:], op=ALU.subtract)
        f = pq[0]
        P.dma("sp", f[:, 0, :], S.PF[o, 0, 2048:2176, cb * 512:(cb + 1) * 512], writes=[f.k(0)])
        for k in range(16):
            P.mm(pN[:], altT[:], zin[:, k, :], k == 0, k == 15, reads=[altT, zin.k(k)], writes=[pN])
        P.X("dve", "tensor_tensor", [pN, f.k(0)], [YN], out=YN[:], in0=pN[:], in1=f[:, 0, :], op=ALU.mult)

    def inverse(epi):
        for i in range(16):
            cmi, smi = load_cs(i)
            y = pY[i % 2]
            for k in range(16):
                P.mm(y[:], cmi[:, k, :], YR[:, k, :], k == 0, False, reads=[cmi, YR.k(k)], writes=[y])
            for k in range(16):
                P.mm(y[:], smi[:, k, :], YW[:, k, :], False, False, reads=[smi, YW.k(k)], writes=[y])
            P.mm(y[:], nyqT[:], YN[:], False, True, reads=[nyqT, YN], writes=[y])
            epi(i, y)

    def load_w(part, cb):
        P.dma("sp", cwt[part][:], C.hy_cw[l][:, part * 2048 + cb * 512:part * 2048 + (cb + 1) * 512].unsqueeze(0).to_broadcast([128, 3, 512]),
              writes=[cwt[part]])
        P.dma("sp", cbt[part][:], C.hy_cb[l][:, part * 2048 + cb * 512:part * 2048 + (cb + 1) * 512].to_broadcast([128, 512]),
              writes=[cbt[part]])

    def vpath(cb):
        load_w(2, cb)
        for i in range(16):
            e = e1[i % 2]
            short_conv(2, cb, i, e)
            P.dma(STQ, S.VC[cb % 2, i * 128:(i + 1) * 128, :], e[:], reads=[e], writes=[("VC", cb % 2, i)])
            P.op("act", lambda en, e=e, i=i: en.copy(out=vz[:, i, :], in_=e[:]), [e], [vz.k(i)])

    vpath(0)
    for cb in range(4):
        for part in range(2):
            load_w(part, cb)
        for o in range(2):
            P.dma("sp", skt[o][:], C.hy_skip[l][o:o + 1, cb * 512:(cb + 1) * 512].to_broadcast([128, 512]), writes=[skt[o]])
        forward(0, cb, vz)
        if cb + 1 < 4:
            vpath(cb + 1)

        def epi1(i, y, cb=cb):
            vc = e1[i % 2]
            x1 = e2[i % 2]
            P.dma("sp", vc[:], S.VC[cb % 2, i * 128:(i + 1) * 128, :], reads=[("VC", cb % 2, i)], writes=[vc])
            short_conv(0, cb, i, x1)
            P.X(PE2, "tensor_tensor", [vc, skt[0]], [vc], out=vc[:], in0=vc[:], in1=skt[0][:], op=ALU.mult)
            P.X("dve", "tensor_tensor", [y, vc], [vc], out=vc[:], in0=y[:], in1=vc[:], op=ALU.add)
            P.X(PE2, "tensor_tensor", [vc, x1], [vc], out=vc[:], in0=vc[:], in1=x1[:], op=ALU.mult)
            P.dma(STQ, S.ZC[i * 128:(i + 1) * 128, :], vc[:], reads=[vc], writes=[("ZC", i)])
            P.op("act", lambda en, vc=vc, i=i: en.copy(out=zz[:, i, :], in_=vc[:]), [vc], [zz.k(i)])

        inverse(epi1)
        forward(1, cb, zz)

        def epi2(i, y, cb=cb):
            zc = e1[i % 2]
            x2 = e2[i % 2]
            ybt = yb[i % 2]
            P.dma("sp", zc[:], S.ZC[i * 128:(i + 1) * 128, :], reads=[("ZC", i)], writes=[zc])
            short_conv(1, cb, i, x2)
            P.X(PE2, "tensor_tensor", [zc, skt[1]], [zc], out=zc[:], in0=zc[:], in1=skt[1][:], op=ALU.mult)
            P.X("dve", "tensor_tensor", [y, zc], [zc], out=zc[:], in0=y[:], in1=zc[:], op=ALU.add)
            P.X(PE2, "tensor_tensor", [zc, x2], [ybt], out=ybt[:], in0=zc[:], in1=x2[:], op=ALU.mult)
            for q in range(4):
                P.op("pe", lambda en, q=q, ybt=ybt: en.transpose(pT[:, q * 128:(q + 1) * 128], ybt[:, q * 128:(q + 1) * 128], ident_b[:]),
                     [ybt, ident_b], [pT])
            o4 = yT4[i % 2]
            evac(P, i, o4[:].rearrange("p q t -> p (q t)"), pT[:], [pT], [o4])
            P.dma(STQ, S.YT[cb * 512:(cb + 1) * 512, t0 + i * 128:t0 + (i + 1) * 128].rearrange("(q p) t -> p q t", p=128), o4[:],
                  reads=[o4], writes=[("YTa", cb, i)])

        inverse(epi2)
    P.end()


_CONSTS = None


def host_consts():
    global _CONSTS
    if _CONSTS is not None:
        return _CONSTS
    import ml_dtypes
    bf = ml_dtypes.bfloat16
    L = SEQ
    pos = np.arange(L, dtype=np.float32)
    t = pos / np.float32(L - 1)
    omega = (np.float32(2.0 * math.pi) * pos / np.float32(L)).astype(np.float32)
    bands = np.linspace(1e-4, 15.0, 16, dtype=np.float32)
    ang = (omega[:, None] * bands[None, :]).astype(np.float32)
    feat = np.concatenate([t[:, None], np.cos(ang), -np.sin(ang)], axis=-1).astype(np.float32)
    deltas = np.abs(np.linspace(math.log(1e-2) / 1.5, math.log(1e-2) / 0.3, 2048, dtype=np.float32))
    win = np.exp(-t[:, None] * deltas[None, :]).astype(np.float32)
    n = np.arange(2048, dtype=np.int64)
    prod = (n[:, None] * n[None, :]) % NFFT
    th = prod.astype(np.float64) * (2.0 * np.pi / NFFT)
    cm = np.cos(th)
    sm = np.sin(th)

    def blk(m):
        return np.ascontiguousarray(m.reshape(16, 128, 16, 128).transpose(2, 1, 0, 3)).astype(bf)

    alt = np.where(np.arange(128) % 2 == 0, 1.0, -1.0).astype(np.float32)
    altT = np.repeat(alt[:, None], 128, axis=1).astype(bf)
    nyq = np.zeros((128, 128), np.float32)
    nyq[0, :] = alt
    tri = np.zeros((64, 2, 64), np.float32)
    ii = np.arange(64)
    tri[:, 0, :] = (ii[:, None] <= ii[None, :])
    tri[:, 1, :] = (ii[:, None] >= ii[None, :])
    _CONSTS = {"c_featT": np.ascontiguousarray(feat.T), "c_win": win, "c_cm": blk(cm), "c_sm": blk(sm),
               "c_alt": altT, "c_nyq": nyq.astype(bf), "c_ident": np.eye(128, dtype=np.float32), "c_tri": tri}
    return _CONSTS


def add_inputs_C(C):
    L = C.L
    C.w_br = [C.inp("w_br_" + n, [L, DMIX, D]) for n in "abc"]
    C.w_out = C.inp("w_out", [L, D, D])
    C.ffn_norm = C.inp("ffn_norm", [L, 128, 32])
    C.w_gate = C.inp("w_gate", [L, D, FFN])
    C.w_up = C.inp("w_up", [L, D, FFN])
    C.w_down = C.inp("w_down", [L, FFN, D])
    C.final_norm = C.inp("final_norm", [1, 128, 32])


def phase_C1(P, C, l, t0):
    S = C.S
    NBC = 256
    P.begin()
    yT = P.sb("yT", [128, 48, 512], BF16)
    yv = S.YT[:, t0:t0 + 512].rearrange("(k p) t -> p k t", p=128)
    for a in range(0, 48, 8):
        P.dma("sp", yT[:, a:a + 8, :], yv[:, a:a + 8, :], writes=[yT.k(a // 8)])
    wb = [[P.sb("wbr", [128, 16, NBC], BF16) for _ in range(2)] for _ in range(3)]
    gt = [P.sb("gt", [128, 3, 512], F32) for _ in range(2)]
    m = [P.sb("m", [128, 512], F32) for _ in range(2)]
    t = [P.sb("t", [128, 512], F32) for _ in range(2)]
    mo = [P.sb("mo", [128, 512], BF16) for _ in range(2)]
    ps = [[P.ps("psb", [128, 512], F32) for _ in range(2)] for _ in range(3)]
    wvs = [C.w_br[br][l].rearrange("(k p) n -> p k n", p=128) for br in range(3)]
    gv = S.GT[:, t0:t0 + 512].rearrange("(b x p) t -> p b x t", b=3, p=128)
    it = 0
    for db in range(D // NBC):
        bufs = []
        for br in range(3):
            b = wb[br][db % 2]
            for a in range(0, 16, 8):
                P.dma("pool", b[:, a:a + 8, :], wvs[br][:, a:a + 8, db * NBC:(db + 1) * NBC], writes=[b.k(a // 8)])
            bufs.append(b)
        for nt in range(NBC // 128):
            dt = db * (NBC // 128) + nt
            g = gt[it % 2]
            P.dma("sp", g[:], gv[:, :, dt, :], writes=[g])
            for br in range(3):
                p_ = ps[br][it % 2]
                for k in range(16):
                    P.mm(p_[:], bufs[br][:, k, nt * 128:(nt + 1) * 128], yT[:, br * 16 + k, :], k == 0, k == 15,
                         reads=[bufs[br].k(k // 8), yT.k((br * 16 + k) // 8)], writes=[p_])
            mm_, tt_, oo_ = m[it % 2], t[it % 2], mo[it % 2]
            P.X("dve", "tensor_tensor", [ps[0][it % 2], g], [mm_], out=mm_[:], in0=ps[0][it % 2][:], in1=g[:, 0, :], op=ALU.mult)
            P.X("dve", "tensor_tensor", [ps[1][it % 2], g], [tt_], out=tt_[:], in0=ps[1][it % 2][:], in1=g[:, 1, :], op=ALU.mult)
            P.X("dve", "tensor_tensor", [mm_, tt_], [mm_], out=mm_[:], in0=mm_[:], in1=tt_[:], op=ALU.add)
            P.X("dve", "tensor_tensor", [ps[2][it % 2], g], [tt_], out=tt_[:], in0=ps[2][it % 2][:], in1=g[:, 2, :], op=ALU.mult)
            P.X("dve", "tensor_tensor", [mm_, tt_], [oo_], out=oo_[:], in0=mm_[:], in1=tt_[:], op=ALU.add)
            P.dma("sp", S.MT[dt * 128:(dt + 1) * 128, t0:t0 + 512], oo_[:], reads=[oo_], writes=[("MT", dt)])
            it += 1
    P.end()


def gemm_residual(P, C, G, xT, xkeys, kc, wv, k0, t0, piece=8):
    S = C.S
    xt = [P.sb("xt", [128, 512], F32) for _ in range(4)]
    cnt = [0]

    def epi(col0, n, tb, ps):
        x = xt[cnt[0] % 4]
        cnt[0] += 1
        dst = S.XT[col0:col0 + n, t0 + tb * 512:t0 + (tb + 1) * 512]
        P.dma("sp", x[0:n, :], dst, reads=[("XT", col0, tb)], writes=[x])
        P.X("dve", "tensor_tensor", [ps, x], [x], out=x[0:n, :], in0=ps[0:n, :], in1=x[0:n, :], op=ALU.add)
        P.dma("sp", dst, x[0:n, :], reads=[x], writes=[("XT", col0, tb)])

    G.run_fm(xT, xkeys, kc, 1024, wv, k0, 0, D, epi, piece=piece)


def phase_C2(P, C, l, t0):
    S = C.S
    P.begin()
    mT = P.sb("mT", [128, 32, 1024], BF16)
    mv = S.MT[:, t0:t0 + 1024].rearrange("(k p) t -> p k t", p=128)
    for a in range(0, 32, 8):
        P.dma("sp", mT[:, a:a + 8, :], mv[:, a:a + 8, :], writes=[mT.k(a // 8)])
    G = Gemm(P, 32, 512, nbuf=2, npsum=4)
    wv = C.w_out[l].rearrange("(k p) n -> p k n", p=128)
    gemm_residual(P, C, G, mT, [mT.k(a) for a in range(4)], 32, wv, 0, t0)
    P.end()


def phase_C3(P, C, l, t0):
    S = C.S
    NBC = 256
    P.begin()
    ones_f = P.sb("ones_f", [128, 128], F32)
    P.X("dve", "memset", [], [ones_f], ap=ones_f[:], constant=1.0)
    C.epsb = P.sb("epsb", [128, 1], F32)
    P.X("dve", "memset", [], [C.epsb], ap=C.epsb[:], constant=EPS)
    wn = P.sb("wn", [128, 32], F32)
    P.dma("sp", wn[:], C.ffn_norm[l], writes=[wn])
    uT = P.sb("uT", [128, 32, 1024], BF16)
    ukeys = [uT.k(k) for k in range(32)]
    banks = [P.ps("bk", [128, 512], F32) for _ in range(8)]
    rms_build(P, C, S.XT, t0, wn, uT, ones_f, pst=banks[0:2])
    wg = [P.sb("wg", [128, 32, NBC], BF16) for _ in range(2)]
    wu = [P.sb("wu", [128, 32, NBC], BF16) for _ in range(2)]
    sg = [P.sb("sg", [128, 512], F32) for _ in range(2)]
    fo = [P.sb("fo", [128, 512], BF16) for _ in range(4)]
    gv = C.w_gate[l].rearrange("(k p) n -> p k n", p=128)
    uv = C.w_up[l].rearrange("(k p) n -> p k n", p=128)
    it = 0
    for fb in range(FFN // NBC):
        bg, bu = wg[fb % 2], wu[fb % 2]
        for a in range(0, 32, 8):
            P.dma("pool", bg[:, a:a + 8, :], gv[:, a:a + 8, fb * NBC:(fb + 1) * NBC], writes=[bg.k(a // 8)])
            P.dma("pool", bu[:, a:a + 8, :], uv[:, a:a + 8, fb * NBC:(fb + 1) * NBC], writes=[bu.k(a // 8)])
        for nt in range(NBC // 128):
            f0 = fb * NBC + nt * 128
            pg = [banks[(it % 2) * 4 + tb] for tb in range(2)]
            pu = [banks[(it % 2) * 4 + 2 + tb] for tb in range(2)]
            for k in range(32):
                for tb in range(2):
                    P.mm(pg[tb][:], bg[:, k, nt * 128:(nt + 1) * 128], uT[:, k, tb * 512:(tb + 1) * 512], k == 0, k == 31,
                         reads=[bg.k(k // 8), uT.k(k)], writes=[pg[tb]])
            for k in range(32):
                for tb in range(2):
                    P.mm(pu[tb][:], bu[:, k, nt * 128:(nt + 1) * 128], uT[:, k, tb * 512:(tb + 1) * 512], k == 0, k == 31,
                         reads=[bu.k(k // 8), uT.k(k)], writes=[pu[tb]])
            for tb in range(2):
                s = sg[tb]
                o = fo[(it % 2) * 2 + tb]
                P.act(s[:], pg[tb][:], AF.Silu, reads=[pg[tb]], writes=[s])
                P.X("dve", "tensor_tensor", [pu[tb], s], [o], out=o[:], in0=pu[tb][:], in1=s[:], op=ALU.mult)
                P.dma("sp", S.FF[f0:f0 + 128, t0 + tb * 512:t0 + (tb + 1) * 512], o[:], reads=[o], writes=[("FF", f0, tb)])
            it += 1
    P.end()


def phase_C4(P, C, l, t0):
    S = C.S
    KH = FFN // 128 // 2
    for kh in range(2):
        P.begin()
        fT = P.sb("fT", [128, KH, 1024], BF16)
        fv = S.FF[kh * KH * 128:(kh + 1) * KH * 128, t0:t0 + 1024].rearrange("(k p) t -> p k t", p=128)
        pieces = list(range(0, KH, 8))
        for a in pieces:
            b = min(KH, a + 8)
            P.dma("sp", fT[:, a:b, :], fv[:, a:b, :], writes=[fT.k(a // 8)])
        G = Gemm(P, KH, 256, nbuf=2, npsum=4)
        wv = C.w_down[l].rearrange("(k p) n -> p k n", p=128)
        gemm_residual(P, C, G, fT, [fT.k(a // 8) for a in pieces], KH, wv, kh * KH, t0)
        P.end()


def phase_F(P, C, t0):
    S = C.S
    P.begin()
    ones_f = P.sb("ones_f", [128, 128], F32)
    P.X("dve", "memset", [], [ones_f], ap=ones_f[:], constant=1.0)
    epsb = P.sb("epsb", [128, 1], F32)
    P.X("dve", "memset", [], [epsb], ap=epsb[:], constant=EPS)
    wn = P.sb("wn", [128, 32], F32)
    P.dma("sp", wn[:], C.final_norm[0], writes=[wn])
    xk = [P.sb("xk", [128, 1024], F32) for _ in range(3)]
    sq = [P.sb("sq", [128, 1024], F32) for _ in range(2)]
    oo = [P.sb("oo", [128, 1024], F32) for _ in range(2)]
    rstd = P.sb("rstd", [128, 1024], F32)
    pst = [P.ps("pst", [128, 512], F32) for _ in range(2)]
    for k in range(32):
        x = xk[k % 3]
        s = sq[k % 2]
        P.dma("sp", x[:], S.XT[k * 128:(k + 1) * 128, t0:t0 + 1024], writes=[x])
        P.act(s[:], x[:], AF.Square, reads=[x], writes=[s])
        for tb in range(2):
            P.mm(pst[tb][:], ones_f[:], s[:, tb * 512:(tb + 1) * 512], k == 0, k == 31, reads=[s, ones_f], writes=[pst[tb]])
    for tb in range(2):
        sl = slice(tb * 512, (tb + 1) * 512)
        P.act(rstd[:, sl], pst[tb][:], AF.Sqrt, reads=[pst[tb], epsb], writes=[rstd.k(tb)], bias=epsb[:, 0:1], scale=1.0 / D)
        P.X("dve", "reciprocal", [rstd.k(tb)], [rstd.k(tb)], out=rstd[:, sl], in_=rstd[:, sl])
    for k in range(32):
        x = xk[k % 3]
        o = oo[k % 2]
        P.dma("sp", x[:], S.XT[k * 128:(k + 1) * 128, t0:t0 + 1024], writes=[x])
        P.X("dve", "scalar_tensor_tensor", [x, wn, rstd.k(0), rstd.k(1)], [o], out=o[:], in0=x[:], scalar=wn[:, k:k + 1],
            in1=rstd[:], op0=ALU.mult, op1=ALU.mult)
        P.dma("sp", C.OUT[k * 128:(k + 1) * 128, t0:t0 + 1024], o[:], reads=[o], writes=[("OUT", k)])
    P.end()


NCORES = 4
NT_CORE = SEQ * (4 // NCORES)
_PROG = {}


def build_full(nt=NT_CORE, depth=DEPTH):
    key = (nt, depth)
    if key in _PROG:
        return _PROG[key]
    nc = bass.Bass("TRN2", target_bir_lowering=False)
    C = make_ctx(nc, nt, depth)
    C.x_T = C.inp("x_T", [D, nt])
    C.OUT = dram(nc, "OUT", [D, nt], F32, "ExternalOutput")
    add_inputs_A(C)
    add_inputs_H(C)
    add_inputs_M(C)
    add_inputs_R(C)
    add_inputs_C(C)
    P = Prog(nc)
    S = C.S
    P.begin()
    for k in range(8):
        P.dma("sp", S.XT[k * 512:(k + 1) * 512, :], C.x_T[k * 512:(k + 1) * 512, :], writes=[("XTinit", k)])
    P.end()
    nb = nt // SEQ
    for l in range(depth):
        for t0 in range(0, nt, 1024):
            phase_A(P, C, l, t0)
        phase_HF(P, C, l)
        for b in range(nb):
            phase_HC(P, C, l, b * SEQ)
            phase_M(P, C, l, b * SEQ)
            phase_R(P, C, l, b * SEQ)
        for t0 in range(0, nt, 1024):
            phase_C1(P, C, l, t0)
            phase_C1(P, C, l, t0 + 512)
            phase_C2(P, C, l, t0)
            phase_C3(P, C, l, t0)
            phase_C4(P, C, l, t0)
    for t0 in range(0, nt, 1024):
        phase_F(P, C, t0)
    P.begin()
    P.end(final=True)
    _PROG[key] = nc
    return nc


def _pk(v):
    lead = v.shape[:-1]
    n = v.shape[-1] // 128
    return np.ascontiguousarray(np.moveaxis(v.reshape(lead + (n, 128)), -1, -2))


def host_params(inp):
    f = lambda a: np.ascontiguousarray(np.asarray(a, dtype=np.float32))
    L = inp["w_in"].shape[0]
    p = {}
    p["mix_norm"] = _pk(f(inp["mix_norm"]))
    p["w_in"] = f(inp["w_in"])
    p["b_gate"] = _pk(f(inp["b_gate"]))
    p["hy_w1"] = f(inp["hy_w1"])
    p["hy_b1"] = f(inp["hy_b1"]).reshape(L, 64, 1)
    p["hy_w2"] = f(inp["hy_w2"])
    p["hy_b2"] = np.ascontiguousarray(f(inp["hy_b2"]).transpose(0, 2, 1))
    p["hy_freq"] = f(inp["hy_freq"]).reshape(L, 64, 1)
    p["hy_w3"] = f(inp["hy_w3"])
    p["hy_cw"] = f(inp["hy_conv_w"])
    p["hy_cb"] = f(inp["hy_conv_b"]).reshape(L, 1, 6144)
    p["hy_skip"] = f(inp["hy_skip"])
    p["ml_gb"] = f(inp["ml_gate_b"]).reshape(L, 1, 32)
    p["ml_norm"] = f(inp["ml_norm"]).reshape(L, 1, 2048)
    p["rg_cw"] = np.ascontiguousarray(f(inp["rg_conv_w"]).reshape(L, 4, 16, 128).transpose(0, 3, 2, 1))
    p["rg_cb"] = _pk(f(inp["rg_conv_b"]))
    for a, b in (("rg_ba", "rg_ba"), ("rg_bx", "rg_bx"), ("rg_lam", "rg_lambda")):
        p[a] = np.ascontiguousarray(f(inp[b]).reshape(L, 2, 16, 128).transpose(0, 3, 1, 2))
    p["rg_wa"] = f(inp["rg_wa"])
    p["rg_wx"] = f(inp["rg_wx"])
    p["w_br_a"] = f(inp["w_br_a"])
    p["w_br_b"] = f(inp["w_br_b"])
    p["w_br_c"] = f(inp["w_br_c"])
    p["w_out"] = f(inp["w_out"])
    p["ffn_norm"] = _pk(f(inp["ffn_norm"]))
    p["w_gate"] = f(inp["w_gate"])
    p["w_up"] = f(inp["w_up"])
    p["w_down"] = f(inp["w_down"])
    p["final_norm"] = _pk(f(inp["final_norm"]).reshape(1, D))
    p.update(host_consts())
    return p


def kernel(**inputs):
    x = np.asarray(inputs["x"], dtype=np.float32)
    nc = build_full()
    params = host_params(inputs)
    bpc = 4 // NCORES
    in_maps = []
    for c in range(NCORES):
        xs = x[c * bpc:(c + 1) * bpc].reshape(bpc * SEQ, D)
        m = dict(params)
        m["x_T"] = np.ascontiguousarray(xs.T)
        in_maps.append(m)
    res = run_bass_kernel_spmd(nc, in_maps, core_ids=list(range(NCORES)))
    outs = [np.ascontiguousarray(r["OUT"].T).reshape(bpc, SEQ, D) for r in res.results]
    return np.concatenate(outs, axis=0).astype(np.float32)
```

```python
import contextlib
import math
import numpy as np
import concourse.bass as bass
import concourse.mybir as mybir
from concourse.bass_utils import run_bass_kernel_spmd

F32 = mybir.dt.float32
BF16 = mybir.dt.bfloat16
AF = mybir.ActivationFunctionType
ALU = mybir.AluOpType
AX = mybir.AxisListType

D = 4096
SEQ = 2048
DEPTH = 2
DMIX = 2048
NHEAD = 8
DK = 128
DV = 256
FFN = 11008
D_IN = 28704
EPS = 1e-6
N_DMA_SEMS = 12
MAGIC = 12582912.0
TWO_PI = float(2 * np.pi)

C_HY = 0
C_Q = 6144
C_K = 7168
C_V = 8192
C_O = 10240
C_MG = 12288
C_RX = 12320
C_RY = 14368
C_G = 16416


class Tile:
    def __init__(self, name, t):
        self.name = name
        self.t = t

    def __getitem__(self, idx):
        return self.t[idx]

    def k(self, *idx):
        return (self.name,) + tuple(idx)


def _key(x):
    if isinstance(x, Tile):
        return (x.name,)
    if isinstance(x, tuple):
        return x
    return (x,)


class Prog:
    ENG = ("pe", "dve", "act", "pool", "sp")

    def __init__(self, nc):
        self.nc = nc
        self.stack = contextlib.ExitStack()
        self.tstack = None
        self.ops = {e: [] for e in self.ENG}
        self.cnt = {}
        self.sems = {}
        self.known = {e: {} for e in self.ENG}
        self.bufs = {}
        for e in self.ENG:
            self.sems[e] = self.stack.enter_context(nc.semaphore("s_" + e))
            self.cnt[e] = 0
        self.dma_names = {}
        self.dma_rr = {}
        for q in ("sp", "pool", "act"):
            names = []
            for i in range(N_DMA_SEMS):
                n = "d_%s%d" % (q, i)
                self.sems[n] = self.stack.enter_context(nc.semaphore(n))
                self.cnt[n] = 0
                names.append(n)
            self.dma_names[q] = names
            self.dma_rr[q] = 0
        self.nphase = 0
        self.uid = 0
        self.psum_names = set()

    def begin(self):
        self.tstack = contextlib.ExitStack()
        if self.nphase > 0:
            for e in self.ENG:
                for s, v in self.cnt.items():
                    if v > 0 and not (s == e and e == "pe") and self.known[e].get(s, 0) < v:
                        self.ops[e].append(("wait", s, v))
                        self.known[e][s] = v
            self.bufs = {}
        self.nphase += 1

    def end(self, final=False):
        nc = self.nc
        if final:
            for s, v in self.cnt.items():
                if s != "sp" and v > 0 and self.known["sp"].get(s, 0) < v:
                    self.ops["sp"].append(("wait", s, v))
        ops = self.ops
        sems = self.sems
        with nc.Block() as block:
            def run(eng, name):
                for o in ops[name]:
                    if o[0] == "wait":
                        eng.wait_ge(sems[o[1]], o[2])
                    else:
                        o[1](eng).then_inc(sems[o[2]], o[3])

            @block.tensor
            def _(e):
                run(e, "pe")

            @block.vector
            def _(e):
                run(e, "dve")

            @block.scalar
            def _(e):
                run(e, "act")

            @block.gpsimd
            def _(e):
                run(e, "pool")

            @block.sync
            def _(e):
                run(e, "sp")
        self.ops = {e: [] for e in self.ENG}
        self.tstack.close()
        self.tstack = None
        if final:
            self.stack.close()

    def sb(self, name, shape, dtype):
        self.uid += 1
        nm = "%s_%d" % (name, self.uid)
        return Tile(nm, self.tstack.enter_context(self.nc.sbuf_tensor(nm, list(shape), dtype)))

    def ps(self, name, shape, dtype=F32):
        self.uid += 1
        nm = "%s_%d" % (name, self.uid)
        self.psum_names.add(nm)
        return Tile(nm, self.tstack.enter_context(self.nc.psum_tensor(nm, list(shape), dtype)))

    def _need(self, eng, waits, sem, val):
        if val <= 0 or (sem == eng and eng == "pe"):
            return
        if self.known[eng].get(sem, 0) >= val:
            return
        waits[sem] = max(waits.get(sem, 0), val)

    def _deps(self, eng, reads, writes, waits=None):
        waits = {} if waits is None else waits
        for r in reads:
            b = self.bufs.get(_key(r))
            if b and b["w"]:
                self._need(eng, waits, *b["w"])
            if b and _key(r)[0] in self.psum_names:
                for s, v in b["r"].items():
                    if s != eng:
                        self._need(eng, waits, s, v)
        for w in writes:
            b = self.bufs.get(_key(w))
            if b:
                if b["w"]:
                    self._need(eng, waits, *b["w"])
                for s, v in b["r"].items():
                    self._need(eng, waits, s, v)
        for s, v in waits.items():
            self.ops[eng].append(("wait", s, v))
            self.known[eng][s] = v

    def _mark(self, reads, writes, sem, val):
        for r in reads:
            b = self.bufs.setdefault(_key(r), {"w": None, "r": {}})
            b["r"][sem] = max(b["r"].get(sem, 0), val)
        for w in writes:
            self.bufs[_key(w)] = {"w": (sem, val), "r": {}}

    def op(self, eng, fn, reads=(), writes=()):
        self._deps(eng, reads, writes)
        self.cnt[eng] += 1
        self.ops[eng].append(("op", fn, eng, 1))
        self._mark(reads, writes, eng, self.cnt[eng])

    def dma(self, q, out, in_, reads=(), writes=()):
        names = self.dma_names[q]
        s = names[self.dma_rr[q] % len(names)]
        self.dma_rr[q] += 1
        waits = {}
        self._need(q, waits, s, self.cnt[s])
        self._deps(q, reads, writes, waits)
        self.cnt[s] += 16
        self.ops[q].append(("op", lambda e: e.dma_start(out=out, in_=in_), s, 16))
        self._mark(reads, writes, s, self.cnt[s])

    def X(self, eng, method, reads, writes, **kw):
        self.op(eng, lambda e: getattr(e, method)(**kw), reads, writes)

    def mm(self, out, lhsT, rhs, start, stop, reads, writes):
        self.op("pe", lambda e: e.matmul(out, lhsT, rhs, start=start, stop=stop), reads, writes)

    def act(self, out, in_, func, reads, writes, bias=None, scale=None):
        kw = {}
        if bias is not None:
            kw["bias"] = bias
        if scale is not None:
            kw["scale"] = scale
        self.op("act", lambda e: e.activation(out=out, in_=in_, func=func, **kw), reads, writes)


class Gemm:
    def __init__(self, P, kc_max, nb, nbuf=2, npsum=4, wdtype=BF16):
        self.P = P
        self.nb = nb
        self.wb = [P.sb("wb", [128, kc_max, nb], wdtype) for _ in range(nbuf)]
        self.pg = [P.ps("pg", [128, 512], F32) for _ in range(npsum)]
        self.wi = 0
        self.pi = 0

    def next_ps(self):
        p = self.pg[self.pi % len(self.pg)]
        self.pi += 1
        return p

    def load_w(self, wv, k0, kc, c0, ncols, q="pool", piece=8):
        P = self.P
        buf = self.wb[self.wi % len(self.wb)]
        self.wi += 1
        for a in range(0, kc, piece):
            b = min(kc, a + piece)
            P.dma(q, buf[:, a:b, 0:ncols], wv[:, k0 + a:k0 + b, c0:c0 + ncols], writes=[buf.k(a)])
        return buf, [buf.k(a) for a in range(0, kc, piece)]

    def run_fm(self, xT, xkeys, kc, ntok, wv, k0, c0, ncols, epi, piece=8):
        P = self.P
        for cb in range(c0, c0 + ncols, self.nb):
            nbc = min(self.nb, c0 + ncols - cb)
            buf, wkeys = self.load_w(wv, k0, kc, cb, nbc, piece=piece)
            for n0 in range(0, nbc, 128):
                n = min(128, nbc - n0)
                pss = [self.next_ps() for _ in range(ntok // 512)]
                for k in range(kc):
                    for tb, ps in enumerate(pss):
                        P.mm(ps[0:n, :], buf[:, k, n0:n0 + n], xT[:, k, tb * 512:(tb + 1) * 512],
                             k == 0, k == kc - 1, reads=[buf.k((k // piece) * piece)] + xkeys, writes=[ps])
                for tb, ps in enumerate(pss):
                    epi(cb + n0, n, tb, ps)

    def run_tm(self, xT, xkeys, kc, ntok, wv, k0, c0, ncols, epi, piece=8):
        P = self.P
        for cb in range(c0, c0 + ncols, self.nb):
            nbc = min(self.nb, c0 + ncols - cb)
            buf, wkeys = self.load_w(wv, k0, kc, cb, nbc, piece=piece)
            for tt in range(ntok // 128):
                ps = self.next_ps()
                for k in range(kc):
                    P.mm(ps[:, 0:nbc], xT[:, k, tt * 128:(tt + 1) * 128], buf[:, k, 0:nbc],
                         k == 0, k == kc - 1, reads=[buf.k((k // piece) * piece)] + xkeys, writes=[ps])
                epi(tt, cb, nbc, ps)


class Ctx:
    pass


def dram(nc, name, shape, dtype, kind="Internal"):
    return nc.dram_tensor(name, list(shape), dtype, kind=kind).ap()


def rms_build(P, C, XT, t0, wn, uT, ones_f, pst=None, rstd_ready=None):
    xk = [P.sb("xk", [128, 1024], F32) for _ in range(3)]
    sq = [P.sb("sq", [128, 1024], F32) for _ in range(2)]
    rstd = P.sb("rstd", [128, 1024], F32)
    if rstd_ready is None:
        own = pst is None
        if own:
            pst = [P.ps("pst", [128, 512], F32) for _ in range(2)]
        for k in range(32):
            x = xk[k % 3]
            s = sq[k % 2]
            P.dma("sp", x[:], XT[k * 128:(k + 1) * 128, t0:t0 + 1024], reads=[("XT", t0 // 1024)], writes=[x])
            P.act(s[:], x[:], AF.Square, reads=[x], writes=[s])
            for tb in range(2):
                P.mm(pst[tb][:], ones_f[:], s[:, tb * 512:(tb + 1) * 512], k == 0, k == 31, reads=[s, ones_f], writes=[pst[tb]])
        for tb in range(2):
            P.act(rstd[:, tb * 512:(tb + 1) * 512], pst[tb][:], AF.Sqrt, reads=[pst[tb], C.epsb], writes=[rstd.k(tb)],
                  bias=C.epsb[:, 0:1], scale=1.0 / D)
            P.X("dve", "reciprocal", [rstd.k(tb)], [rstd.k(tb)], out=rstd[:, tb * 512:(tb + 1) * 512], in_=rstd[:, tb * 512:(tb + 1) * 512])
    else:
        pst = rstd_ready
        for tb in range(2):
            P.act(rstd[:, tb * 512:(tb + 1) * 512], pst[tb][:], AF.Sqrt, reads=[pst[tb], C.epsb], writes=[rstd.k(tb)],
                  bias=C.epsb[:, 0:1], scale=1.0 / D)
            P.X("dve", "reciprocal", [rstd.k(tb)], [rstd.k(tb)], out=rstd[:, tb * 512:(tb + 1) * 512], in_=rstd[:, tb * 512:(tb + 1) * 512])
    for k in range(32):
        x = xk[k % 3]
        P.dma("sp", x[:], XT[k * 128:(k + 1) * 128, t0:t0 + 1024], reads=[("XT", t0 // 1024)], writes=[x])
        P.X("dve", "scalar_tensor_tensor", [x, wn, rstd.k(0), rstd.k(1)], [uT.k(k)],
            out=uT[:, k, :], in0=x[:], scalar=wn[:, k:k + 1], in1=rstd[:], op0=ALU.mult, op1=ALU.mult)
    return rstd


def evac(P, i, out, in_, reads, writes):
    if i % 2 == 0:
        P.op("act", lambda e: e.copy(out=out, in_=in_), reads, writes)
    else:
        P.X("dve", "tensor_copy", reads, writes, out=out, in_=in_)


def phase_A(P, C, l, t0):
    S = C.S
    P.begin()
    ones_f = P.sb("ones_f", [128, 128], F32)
    P.X("dve", "memset", [], [ones_f], ap=ones_f[:], constant=1.0)
    C.epsb = P.sb("epsb", [128, 1], F32)
    P.X("dve", "memset", [], [C.epsb], ap=C.epsb[:], constant=EPS)
    wn = P.sb("wn", [128, 32], F32)
    P.dma("sp", wn[:], C.mix_norm[l], writes=[wn])
    bg = P.sb("bg", [128, 96], F32)
    P.dma("sp", bg[:], C.b_gate[l], writes=[bg])
    uT = P.sb("uT", [128, 32, 1024], BF16)
    ukeys = [uT.k(k) for k in range(32)]
    G = Gemm(P, 32, 512, nbuf=2, npsum=4)
    rms_build(P, C, S.XT, t0, wn, uT, ones_f)
    wv = C.w_in[l].rearrange("(k p) n -> p k n", p=128)
    st = [P.sb("st", [128, 512], F32) for _ in range(4)]
    cnt = [0]

    def tm_epi(dst, dcol0, func=None):
        def epi(tt, col0, n, ps):
            i = cnt[0]
            cnt[0] += 1
            s = st[i % 4]
            if func is None:
                evac(P, i, s[:, 0:n], ps[:, 0:n], [ps], [s])
            else:
                P.act(s[:, 0:n], ps[:, 0:n], func, reads=[ps], writes=[s])
            c = col0 - dcol0
            P.dma("sp", dst[t0 + tt * 128:t0 + (tt + 1) * 128, c:c + n], s[:, 0:n], reads=[s], writes=[("A_out", i)])
        return epi

    def fm_epi(dst, dcol0, func=None, bias_of=None):
        def epi(col0, n, tb, ps):
            i = cnt[0]
            cnt[0] += 1
            s = st[i % 4]
            c = col0 - dcol0
            if func is None:
                evac(P, i, s[0:n, :], ps[0:n, :], [ps], [s])
            elif bias_of is None:
                P.act(s[0:n, :], ps[0:n, :], func, reads=[ps], writes=[s])
            else:
                kk = c // 128
                P.act(s[0:n, :], ps[0:n, :], func, reads=[ps, bias_of], writes=[s], bias=bias_of[0:n, kk:kk + 1])
            P.dma("sp", dst[c:c + n, t0 + tb * 512:t0 + (tb + 1) * 512], s[0:n, :], reads=[s], writes=[("A_out", i)])
        return epi

    G.run_tm(uT, ukeys, 32, 1024, wv, 0, C_HY, 6144, tm_epi(S.HY, C_HY))
    G.run_tm(uT, ukeys, 32, 1024, wv, 0, C_Q, 2048, tm_epi(S.QK, C_Q))
    G.run_tm(uT, ukeys, 32, 1024, wv, 0, C_V, 2048, tm_epi(S.VV, C_V))
    G.run_tm(uT, ukeys, 32, 1024, wv, 0, C_O, 2048, tm_epi(S.OO, C_O, AF.Sigmoid))
    G.run_tm(uT, ukeys, 32, 1024, wv, 0, C_MG, 32, tm_epi(S.MG, C_MG))
    G.run_fm(uT, ukeys, 32, 1024, wv, 0, C_RX, 2048, fm_epi(S.RGX, C_RX))
    G.run_fm(uT, ukeys, 32, 1024, wv, 0, C_RY, 2048, fm_epi(S.RGY, C_RY, AF.Gelu))
    G.run_fm(uT, ukeys, 32, 1024, wv, 0, C_G, 12288, fm_epi(S.GT, C_G, AF.Sigmoid, bg))
    P.end()


def make_ctx(nc, NT, L, kinds=None, consts_kind="ExternalInput"):
    kinds = kinds or {}
    C = Ctx()
    S = Ctx()
    C.S = S
    C.NT = NT

    def sc(name, shape, dt):
        return dram(nc, name, shape, dt, kinds.get(name, "Internal"))

    def inp(name, shape, dt=F32):
        return dram(nc, name, shape, dt, "ExternalInput")

    S.XT = sc("XT", [D, NT], F32)
    S.HY = sc("HY", [NT, 6144], F32)
    S.QK = sc("QK", [NT, 2048], F32)
    S.VV = sc("VV", [NT, 2048], F32)
    S.OO = sc("OO", [NT, 2048], F32)
    S.MG = sc("MG", [NT, 32], F32)
    S.RGX = sc("RGX", [2048, NT], F32)
    S.RGY = sc("RGY", [2048, NT], F32)
    S.GT = sc("GT", [12288, NT], F32)
    S.YT = sc("YT", [6144, NT], BF16)
    S.MT = sc("MT", [D, NT], BF16)
    S.FF = sc("FF", [FFN, NT], BF16)
    C.inp = inp
    C.sc = sc
    C.L = L
    return C


def add_inputs_A(C):
    L = C.L
    C.mix_norm = C.inp("mix_norm", [L, 128, 32])
    C.w_in = C.inp("w_in", [L, D, D_IN])
    C.b_gate = C.inp("b_gate", [L, 128, 96])


def add_inputs_R(C):
    L = C.L
    C.rg_cw = C.inp("rg_cw", [L, 128, 16, 4])
    C.rg_cb = C.inp("rg_cb", [L, 128, 16])
    C.rg_ba = C.inp("rg_ba", [L, 128, 2, 16])
    C.rg_bx = C.inp("rg_bx", [L, 128, 2, 16])
    C.rg_lam = C.inp("rg_lam", [L, 128, 2, 16])
    C.rg_wa = C.inp("rg_wa", [L, 2, 8, 256, 256])
    C.rg_wx = C.inp("rg_wx", [L, 2, 8, 256, 256])


def phase_R(P, C, l, t0):
    S = C.S
    T = SEQ
    P.begin()
    cw = P.sb("cw", [128, 16, 4], F32)
    cb = P.sb("cb", [128, 16], F32)
    ba = P.sb("ba", [128, 2, 16], F32)
    bx = P.sb("bx", [128, 2, 16], F32)
    lam = P.sb("lam", [128, 2, 16], F32)
    c1 = P.sb("c1", [128, 2, 16], F32)
    for t, src in ((cw, C.rg_cw), (cb, C.rg_cb), (ba, C.rg_ba), (bx, C.rg_bx), (lam, C.rg_lam)):
        P.dma("sp", t[:], src[l], writes=[t])
    P.act(c1[:], lam[:], AF.Exp, reads=[lam], writes=[c1], scale=-1.0)
    P.act(c1[:], c1[:], AF.Ln, reads=[c1], writes=[c1], bias=1.0)
    P.X("dve", "tensor_scalar", [c1], [c1], out=c1[:], in0=c1[:], scalar1=-8.0, scalar2=None, op0=ALU.mult)
    wbf = [[P.sb("rgw", [128, 2, 256], BF16) for _ in range(4)] for _ in range(2)]
    xr = [P.sb("xr", [128, T + 3], F32) for _ in range(2)]
    xc = [P.sb("xc", [128, T], F32) for _ in range(2)]
    xcb = [P.sb("xcb", [128, T], BF16) for _ in range(2)]
    r_t = P.sb("r_t", [128, T], F32)
    ig_t = P.sb("ig_t", [128, T], F32)
    a_t = P.sb("a_t", [128, T], F32)
    b_t = P.sb("b_t", [128, T], F32)
    tmp = P.sb("tmp", [128, T], F32)
    hf = P.sb("hf", [128, T], F32)
    hb = P.sb("hb", [128, T], F32)
    yr = P.sb("yr", [128, T], F32)
    yo = [P.sb("yo", [128, T], BF16) for _ in range(2)]
    psa = [P.ps("psa", [128, 512], F32) for _ in range(4)]
    psx = [P.ps("psx", [128, 512], F32) for _ in range(4)]
    for i in range(2):
        P.X("dve", "memset", [], [xr[i]], ap=xr[i][:, 0:2], constant=0.0)
        P.X("dve", "memset", [], [xr[i]], ap=xr[i][:, T + 2:T + 3], constant=0.0)
    for h in range(NHEAD):
        ws = wbf[h % 2]
        for d in range(2):
            P.dma("pool", ws[d][:], C.rg_wa[l, d, h].rearrange("(it p) j -> p it j", p=128), writes=[ws[d]])
            P.dma("pool", ws[2 + d][:], C.rg_wx[l, d, h].rearrange("(it p) j -> p it j", p=128), writes=[ws[2 + d]])
        for it in range(2):
            ct = h * 2 + it
            P.dma("sp", xr[it][:, 2:T + 2], S.RGX[ct * 128:(ct + 1) * 128, t0:t0 + T], writes=[xr[it]])
            P.X("dve", "tensor_scalar", [xr[it], cw, cb], [xc[it]], out=xc[it][:], in0=xr[it][:, 0:T],
                scalar1=cw[:, ct, 0:1], scalar2=cb[:, ct:ct + 1], op0=ALU.mult, op1=ALU.add)
            for k in range(1, 4):
                P.X("dve", "scalar_tensor_tensor", [xr[it], cw, xc[it]], [xc[it]], out=xc[it][:], in0=xr[it][:, k:k + T],
                    scalar=cw[:, ct, k:k + 1], in1=xc[it][:], op0=ALU.mult, op1=ALU.add)
            P.op("act", lambda e, it=it: e.copy(out=xcb[it][:], in_=xc[it][:]), [xc[it]], [xcb[it]])
        for jt in range(2):
            ct = h * 2 + jt
            for d in range(2):
                for tb in range(4):
                    for it in range(2):
                        P.mm(psa[tb][:], ws[d][:, it, jt * 128:(jt + 1) * 128], xcb[it][:, tb * 512:(tb + 1) * 512],
                             it == 0, it == 1, reads=[ws[d], xcb[it]], writes=[psa[tb]])
                    for it in range(2):
                        P.mm(psx[tb][:], ws[2 + d][:, it, jt * 128:(jt + 1) * 128], xcb[it][:, tb * 512:(tb + 1) * 512],
                             it == 0, it == 1, reads=[ws[2 + d], xcb[it]], writes=[psx[tb]])
                for tb in range(4):
                    sl = slice(tb * 512, (tb + 1) * 512)
                    P.act(r_t[:, sl], psa[tb][:], AF.Sigmoid, reads=[psa[tb], ba], writes=[r_t], bias=ba[:, d, ct:ct + 1])
                    P.act(ig_t[:, sl], psx[tb][:], AF.Sigmoid, reads=[psx[tb], bx], writes=[ig_t], bias=bx[:, d, ct:ct + 1])
                P.act(a_t[:], r_t[:], AF.Exp, reads=[r_t, c1], writes=[a_t], scale=c1[:, d, ct:ct + 1])
                P.X("pool", "tensor_tensor", [a_t], [tmp], out=tmp[:], in0=a_t[:], in1=a_t[:], op=ALU.mult)
                P.act(tmp[:], tmp[:], AF.Sqrt, reads=[tmp], writes=[tmp], bias=1.0, scale=-1.0)
                P.X("pool", "tensor_tensor", [tmp, ig_t], [b_t], out=b_t[:], in0=tmp[:], in1=ig_t[:], op=ALU.mult)
                P.X("dve", "tensor_tensor", [b_t, xc[jt]], [b_t], out=b_t[:], in0=b_t[:], in1=xc[jt][:], op=ALU.mult)
                if d == 0:
                    P.X("dve", "tensor_tensor_scan", [a_t, b_t], [hf], out=hf[:], data0=a_t[:], data1=b_t[:],
                        initial=0.0, op0=ALU.mult, op1=ALU.add)
                else:
                    P.X("dve", "tensor_tensor_scan", [a_t, b_t], [hb], out=hb[:, ::-1], data0=a_t[:, ::-1], data1=b_t[:, ::-1],
                        initial=0.0, op0=ALU.mult, op1=ALU.add)
            P.dma("sp", yr[:], S.RGY[ct * 128:(ct + 1) * 128, t0:t0 + T], writes=[yr])
            P.X("pool", "tensor_tensor", [hf, hb], [hf], out=hf[:], in0=hf[:], in1=hb[:], op=ALU.add)
            o = yo[jt]
            P.X("dve", "tensor_tensor", [hf, yr], [o], out=o[:], in0=hf[:], in1=yr[:], op=ALU.mult)
            P.dma("sp", S.YT[4096 + ct * 128:4096 + (ct + 1) * 128, t0:t0 + T], o[:], reads=[o], writes=[("YTc", ct)])
    P.end()


def add_inputs_M(C):
    L = C.L
    if not hasattr(C, "c_ident"):
        C.c_ident = C.inp("c_ident", [128, 128])
    C.c_tri = C.inp("c_tri", [64, 2, 64])
    C.ml_gb = C.inp("ml_gb", [L, 1, 32])
    C.ml_norm = C.inp("ml_norm", [L, 1, 2048])


def phase_M(P, C, l, t0):
    S = C.S
    NCH = 32
    P.begin()
    ident_f = P.sb("ident_f", [128, 128], F32)
    ident_b = P.sb("ident_b", [128, 128], BF16)
    tri = P.sb("tri", [64, 2, 64], F32)
    ones64 = P.sb("ones64", [64, 128], F32)
    epsb = P.sb("epsb", [128, 1], F32)
    P.dma("sp", ident_f[:], C.c_ident, writes=[ident_f])
    P.dma("sp", tri[:], C.c_tri, writes=[tri])
    P.X("dve", "tensor_copy", [ident_f], [ident_b], out=ident_b[:], in_=ident_f[:])
    P.X("dve", "memset", [], [ones64], ap=ones64[:], constant=1.0)
    P.X("dve", "memset", [], [epsb], ap=epsb[:], constant=EPS)
    mgraw = P.sb("mgraw", [64, NCH, 32], F32)
    gb = P.sb("gb", [64, 32], F32)
    mln = P.sb("mln", [64, 2048], F32)
    P.dma("sp", mgraw[:], S.MG[t0:t0 + SEQ, :].rearrange("(c j) k -> j c k", j=64), writes=[mgraw])
    P.dma("sp", gb[:], C.ml_gb[l].to_broadcast([64, 32]), writes=[gb])
    P.dma("sp", mln[:], C.ml_norm[l].to_broadcast([64, 2048]), writes=[mln])
    G = P.sb("G", [64, 32, NCH], F32)
    P.X("dve", "tensor_tensor", [mgraw, gb], [G], out=G[:], in0=mgraw[:].rearrange("p c k -> p k c"),
        in1=gb[:].unsqueeze(2).to_broadcast([64, 32, NCH]), op=ALU.add)
    Gv = G[:].rearrange("p (d g h) c -> p d g h c", d=2, g=2)
    spt = P.sb("spt", [64, 2, 8, NCH], F32)
    P.act(spt[:], Gv[:, :, 1], AF.Exp, reads=[G], writes=[spt], scale=-1.0)
    P.act(spt[:], spt[:], AF.Ln, reads=[spt], writes=[spt], bias=1.0)
    banks = [P.ps("bank", [128, 512], F32) for _ in range(6)]
    tps = [P.ps("tps", [128, 512], BF16) for _ in range(2)]
    eb = P.sb("eb", [64, 2, 256], F32)
    imb = P.sb("imb", [64, 2, 256], F32)
    eib = P.sb("eib", [64, 2, 256], F32)
    ekk = P.sb("ekk", [64, 2, 256], F32)
    eg = P.sb("eg", [128, 2, 256], F32)
    for d in range(2):
        pb = banks[d]
        pg = banks[2 + d]
        rhs = spt[:, d].rearrange("p h c -> p (h c)")
        P.mm(pb[0:64, 0:256], tri[:, d, :], rhs, True, True, reads=[tri, spt], writes=[pb])
        P.mm(pg[:, 0:256], ones64[:], rhs, True, True, reads=[ones64, spt], writes=[pg])
        P.act(eb[:, d, :], pb[0:64, 0:256], AF.Exp, reads=[pb], writes=[eb], scale=-1.0)
        P.X("dve", "tensor_tensor", [G, pb, eb], [imb], out=imb[:, d, :].rearrange("p (h c) -> p h c", h=8),
            in0=Gv[:, d, 0], in1=pb[0:64, 0:256].rearrange("p (h c) -> p h c", h=8), op=ALU.add)
        P.act(eib[:, d, :], imb[:, d, :], AF.Exp, reads=[imb], writes=[eib])
        P.X("dve", "tensor_tensor", [imb, pg], [ekk], out=ekk[:, d, :], in0=imb[:, d, :], in1=pg[0:64, 0:256], op=ALU.subtract)
        P.act(ekk[:, d, :], ekk[:, d, :], AF.Exp, reads=[ekk], writes=[ekk])
        P.act(eg[:, d, :], pg[:, 0:256], AF.Exp, reads=[pg], writes=[eg], scale=-1.0)
    qraw = P.sb("qraw", [64, NCH, 128], F32)
    kraw = P.sb("kraw", [64, NCH, 128], F32)
    vaug = P.sb("vaug", [64, NCH, 260], BF16)
    qs1 = P.sb("qs", [64, NCH, 128], BF16)
    ks1 = P.sb("ks", [64, NCH, 128], BF16)
    qs = [qs1, qs1]
    ks = [ks1, ks1]
    kk = [P.sb("kk", [64, NCH, 128], BF16) for _ in range(2)]
    qT = [P.sb("qT", [128, SEQ], BF16) for _ in range(2)]
    kT = [P.sb("kT", [128, SEQ], BF16) for _ in range(2)]
    CTf = [P.sb("CTf", [128, 257], F32) for _ in range(2)]
    CTall = P.sb("CTall", [128, NCH, 258], BF16)
    STs = [P.sb("STs", [64, 64], BF16) for _ in range(4)]
    HS = P.sb("HS", [64, NCH, 256], F32)
    dn = [P.sb("dn", [64, 1], F32) for _ in range(4)]
    sq = P.sb("sq", [64, 8, 256], F32)
    so = P.sb("so", [64, 8, 256], F32)
    ss = P.sb("ss", [64, 8], F32)
    ytm = P.sb("ytm", [64, 8, 256], BF16)
    yTb = [P.sb("yTb", [128, 512], BF16) for _ in range(2)]
    P.X("dve", "memset", [], [vaug.k("one")], ap=vaug[:, :, 256:257], constant=1.0)
    sc = float(DK) ** -0.5
    for h in range(NHEAD):
        tok = S.QK[t0:t0 + SEQ, :].rearrange("(c j) f -> j c f", j=64)
        P.dma("sp", qraw[:], tok[:, :, h * 128:(h + 1) * 128], writes=[qraw])
        P.dma("sp", kraw[:], tok[:, :, 1024 + h * 128:1024 + (h + 1) * 128], writes=[kraw])
        P.dma("pool", vaug[:, :, 0:256], S.VV[t0:t0 + SEQ, :].rearrange("(c j) f -> j c f", j=64)[:, :, h * 256:(h + 1) * 256],
              writes=[vaug])
        for d in range(2):
            hs = slice(h * NCH, (h + 1) * NCH)
            bc = lambda t: t[:, d, hs].unsqueeze(2).to_broadcast([64, NCH, 128])
            P.X("dve", "tensor_tensor", [qraw, eb], [qs[d]], out=qs[d][:], in0=qraw[:], in1=bc(eb), op=ALU.mult)
            P.X("dve", "scalar_tensor_tensor", [kraw, eib], [ks[d]], out=ks[d][:], in0=kraw[:], scalar=sc, in1=bc(eib),
                op0=ALU.mult, op1=ALU.mult)
            P.X("dve", "scalar_tensor_tensor", [kraw, ekk], [kk[d]], out=kk[d][:], in0=kraw[:], scalar=sc, in1=bc(ekk),
                op0=ALU.mult, op1=ALU.mult)
            ti = 0
            for src, dst in ((qs[d], qT[d]), (ks[d], kT[d])):
                for c8 in range(4):
                    tp = tps[ti % 2]
                    ti += 1
                    for cc in range(8):
                        c = c8 * 8 + cc
                        P.op("pe", lambda e, tp=tp, src=src, c=c, cc=cc: e.transpose(tp[:, cc * 64:(cc + 1) * 64], src[:, c, :], ident_b[0:64, 0:64]),
                             [src, ident_b], [tp])
                    evac(P, ti, dst[:, c8 * 512:(c8 + 1) * 512], tp[:], [tp], [dst.k(c8)])
        for d in range(2):
            for s in range(NCH - 1):
                c = s if d == 0 else NCH - 1 - s
                dps = banks[4 + (s % 2)]
                cur = CTf[s % 2]
                prev = CTf[(s + 1) % 2]
                P.mm(dps[:, 0:257], kk[d][:, c, :], vaug[:, c, 0:257], True, True, reads=[kk[d], vaug, vaug.k("one")], writes=[dps])
                if s == 0:
                    P.X("dve", "tensor_copy", [dps], [cur], out=cur[:], in_=dps[:, 0:257])
                else:
                    col = h * NCH + c
                    P.X("dve", "scalar_tensor_tensor", [prev, eg, dps], [cur], out=cur[:], in0=prev[:],
                        scalar=eg[:, d, col:col + 1], in1=dps[:, 0:257], op0=ALU.mult, op1=ALU.add)
                P.op("act", lambda e, cur=cur, s=s: e.copy(out=CTall[:, s, 0:257], in_=cur[:]), [cur], [CTall.k(s)])
            for s in range(NCH):
                c = s if d == 0 else NCH - 1 - s
                cs = slice(c * 64, (c + 1) * 64)
                c8 = c // 8
                stp = banks[s % 2]
                ops_ = banks[2 + (s % 2)]
                sts = STs[s % 4]
                P.mm(stp[0:64, 0:64], kT[d][:, cs], qT[d][:, cs], True, True, reads=[kT[d].k(c8), qT[d].k(c8)], writes=[stp])
                P.X("dve", "tensor_tensor", [stp, tri], [sts], out=sts[:], in0=stp[0:64, 0:64], in1=tri[:, d, :], op=ALU.mult)
                if s > 0:
                    P.mm(ops_[0:64, 0:257], qT[d][:, cs], CTall[:, s - 1, 0:257], True, False, reads=[qT[d].k(c8), CTall.k(s - 1)], writes=[ops_])
                P.mm(ops_[0:64, 0:257], sts[:], vaug[:, c, 0:257], s == 0, True, reads=[sts, vaug, vaug.k("one")], writes=[ops_])
                dd = dn[s % 4]
                P.op("act", lambda e, dd=dd, ops_=ops_: e.activation(out=dd[:], in_=ops_[0:64, 256:257], func=AF.Abs), [ops_], [dd])
                P.X("dve", "tensor_scalar", [dd], [dd], out=dd[:], in0=dd[:], scalar1=1.0, scalar2=None, op0=ALU.max)
                P.X("dve", "reciprocal", [dd], [dd], out=dd[:], in_=dd[:])
                if d == 0:
                    P.X("dve", "tensor_scalar", [ops_, dd], [HS.k(c)], out=HS[:, c, :], in0=ops_[0:64, 0:256],
                        scalar1=dd[:, 0:1], scalar2=None, op0=ALU.mult)
                else:
                    P.X("dve", "scalar_tensor_tensor", [ops_, dd, HS.k(c)], [HS.k(c)], out=HS[:, c, :], in0=ops_[0:64, 0:256],
                        scalar=dd[:, 0:1], in1=HS[:, c, :], op0=ALU.mult, op1=ALU.add)
        for cg in range(4):
            cr = slice(cg * 8, (cg + 1) * 8)
            hk = [HS.k(c) for c in range(cg * 8, cg * 8 + 8)]
            P.dma("sp", so[:], S.OO[t0 + cg * 512:t0 + (cg + 1) * 512, h * 256:(h + 1) * 256].rearrange("(c j) f -> j c f", j=64), writes=[so])
            P.X("pool", "tensor_tensor", hk, [sq], out=sq[:], in0=HS[:, cr, :], in1=HS[:, cr, :], op=ALU.mult)
            P.X("dve", "tensor_reduce", [sq], [ss], out=ss[:], in_=sq[:], axis=AX.X, op=ALU.add)
            P.act(ss[:], ss[:], AF.Sqrt, reads=[ss, epsb], writes=[ss], bias=epsb[0:64, 0:1], scale=1.0 / DV)
            P.X("dve", "reciprocal", [ss], [ss], out=ss[:], in_=ss[:])
            P.X("dve", "tensor_tensor", hk + [ss], [sq], out=sq[:], in0=HS[:, cr, :], in1=ss[:].unsqueeze(2).to_broadcast([64, 8, 256]), op=ALU.mult)
            P.X("pool", "tensor_tensor", [sq, mln], [sq], out=sq[:], in0=sq[:],
                in1=mln[:, h * 256:(h + 1) * 256].unsqueeze(1).to_broadcast([64, 8, 256]), op=ALU.mult)
            P.X("dve", "tensor_tensor", [sq, so], [ytm], out=ytm[:], in0=sq[:], in1=so[:], op=ALU.mult)
            for vt in range(2):
                tp = tps[vt]
                for cc in range(8):
                    P.op("pe", lambda e, tp=tp, cc=cc, vt=vt: e.transpose(tp[:, cc * 64:(cc + 1) * 64], ytm[:, cc, vt * 128:(vt + 1) * 128], ident_b[0:64, 0:64]),
                         [ytm, ident_b], [tp])
                o = yTb[vt]
                evac(P, vt, o[:], tp[:], [tp], [o])
                r0 = 2048 + h * 256 + vt * 128
                P.dma(STQ, S.YT[r0:r0 + 128, t0 + cg * 512:t0 + (cg + 1) * 512], o[:], reads=[o], writes=[("YTb", h, cg, vt)])
    P.end()


NFFT = 2 * SEQ
CSQ = "act"
STQ = "pool"
PE2 = "dve"
NOCS = False


def add_inputs_H(C):
    L = C.L
    if not hasattr(C, "c_ident"):
        C.c_ident = C.inp("c_ident", [128, 128])
    C.c_featT = C.inp("c_featT", [33, SEQ])
    C.c_win = C.inp("c_win", [SEQ, 2048])
    C.c_cm = C.inp("c_cm", [16, 128, 16, 128], BF16)
    C.c_sm = C.inp("c_sm", [16, 128, 16, 128], BF16)
    C.c_alt = C.inp("c_alt", [128, 128], BF16)
    C.c_nyq = C.inp("c_nyq", [128, 128], BF16)
    C.hy_w1 = C.inp("hy_w1", [L, 33, 64])
    C.hy_b1 = C.inp("hy_b1", [L, 64, 1])
    C.hy_w2 = C.inp("hy_w2", [L, 2, 64, 64])
    C.hy_b2 = C.inp("hy_b2", [L, 64, 2])
    C.hy_freq = C.inp("hy_freq", [L, 64, 1])
    C.hy_w3 = C.inp("hy_w3", [L, 64, 8192])
    C.hy_cw = C.inp("hy_cw", [L, 3, 6144])
    C.hy_cb = C.inp("hy_cb", [L, 1, 6144])
    C.hy_skip = C.inp("hy_skip", [L, 2, 2048])
    S = C.S
    S.PF = C.sc("PF", [2, 2, 17 * 128, 2048], F32)
    S.VC = C.sc("VC", [2, SEQ, 512], F32)
    S.ZC = C.sc("ZC", [SEQ, 512], F32)


def phase_HF(P, C, l):
    S = C.S
    P.begin()
    featT = P.sb("featT", [33, SEQ], F32)
    w1 = P.sb("w1", [33, 64], F32)
    w2 = P.sb("w2", [64, 2, 64], F32)
    b1 = P.sb("b1", [64, 1], F32)
    b2 = P.sb("b2", [64, 2], F32)
    fq = P.sb("fq", [64, 1], F32)
    fb = P.sb("fb", [64, 3], F32)
    w3 = P.sb("w3", [64, 8192], F32)
    ones_f = P.sb("ones_f", [128, 128], F32)
    altT = P.sb("altT", [128, 128], BF16)
    P.dma("sp", featT[:], C.c_featT, writes=[featT])
    P.dma("sp", w1[:], C.hy_w1[l], writes=[w1])
    P.dma("sp", w2[:], C.hy_w2[l].rearrange("j i o -> i j o"), writes=[w2])
    P.dma("sp", b1[:], C.hy_b1[l], writes=[b1])
    P.dma("sp", b2[:], C.hy_b2[l], writes=[b2])
    P.dma("sp", fq[:], C.hy_freq[l], writes=[fq])
    P.dma("sp", w3[:], C.hy_w3[l], writes=[w3])
    P.dma("sp", altT[:], C.c_alt, writes=[altT])
    P.X("dve", "memset", [], [ones_f], ap=ones_f[:], constant=1.0)
    P.X("dve", "tensor_scalar", [b1, fq], [fb], out=fb[:, 0:1], in0=b1[:], scalar1=fq[:, 0:1], scalar2=None, op0=ALU.mult)
    P.X("dve", "tensor_scalar", [b2, fq, fb], [fb], out=fb[:, 1:3], in0=b2[:], scalar1=fq[:, 0:1], scalar2=None, op0=ALU.mult)
    hT = [P.sb("hT", [64, SEQ], F32) for _ in range(2)]
    rr = P.sb("rr", [64, 512], F32)
    r2 = P.sb("r2", [64, 512], F32)
    pm = [P.ps("pm", [128, 512], F32) for _ in range(2)]
    pf = [P.ps("pf", [128, 512], F32) for _ in range(2)]
    pbk = [P.ps("pbk", [128, 512], F32) for _ in range(2)]
    psn = P.ps("psn", [128, 512], F32)
    pnq = P.ps("pnq", [128, 512], F32)
    for layer in range(3):
        src = featT if layer == 0 else hT[(layer - 1) % 2]
        dst = hT[layer % 2]
        for tb in range(4):
            ps = pm[tb % 2]
            sl = slice(tb * 512, (tb + 1) * 512)
            if layer == 0:
                P.mm(ps[0:64, :], w1[:], featT[:, sl], True, True, reads=[w1, featT], writes=[ps])
            else:
                P.mm(ps[0:64, :], w2[:, layer - 1, :], src[:, sl], True, True, reads=[w2, src], writes=[ps])
            P.X("dve", "tensor_scalar", [ps, fq, fb], [rr], out=rr[:], in0=ps[0:64, :], scalar1=fq[:, 0:1],
                scalar2=fb[:, layer:layer + 1], op0=ALU.mult, op1=ALU.add)
            P.X("dve", "tensor_scalar", [rr], [r2], out=r2[:], in0=rr[:], scalar1=1.0 / TWO_PI, scalar2=MAGIC, op0=ALU.mult, op1=ALU.add)
            P.X("dve", "tensor_scalar", [r2], [r2], out=r2[:], in0=r2[:], scalar1=MAGIC, scalar2=TWO_PI, op0=ALU.subtract, op1=ALU.mult)
            P.X("dve", "tensor_tensor", [rr, r2], [rr], out=rr[:], in0=rr[:], in1=r2[:], op=ALU.subtract)
            P.X("dve", "tensor_scalar", [rr], [rr], out=rr[:], in0=rr[:], scalar1=float(np.pi), scalar2=-float(np.pi), op0=ALU.min, op1=ALU.max)
            P.act(dst[:, sl], rr[:], AF.Sin, reads=[rr], writes=[dst])
    h3 = P.sb("h3b", [64, SEQ], BF16)
    w3b = P.sb("w3b", [64, 8192], BF16)
    P.X("dve", "tensor_copy", [hT[0]], [h3], out=h3[:], in_=hT[0][:])
    P.X("pool", "tensor_copy", [w3], [w3b], out=w3b[:], in_=w3[:])
    nacc = P.sb("nacc", [128, 512], F32)
    sumt = P.sb("sumt", [128, 16, 512], BF16)
    dift = P.sb("dift", [128, 16, 512], BF16)
    win = [P.sb("win", [128, 512], F32) for _ in range(2)]
    fw = [P.sb("fw", [128, 512], F32) for _ in range(2)]
    bw = [P.sb("bw", [128, 512], F32) for _ in range(2)]
    s1 = [P.sb("s1", [128, 512], F32) for _ in range(2)]
    s2 = [P.sb("s2", [128, 512], F32) for _ in range(2)]
    rs2 = P.sb("rs2", [128, 512], F32)
    rsN = P.sb("rsN", [128, 512], F32)
    cmb = [P.sb("cmb", [128, 16, 128], BF16) for _ in range(3)]
    smb = [P.sb("smb", [128, 16, 128], BF16) for _ in range(3)]
    po = [P.sb("po", [128, 512], F32) for _ in range(4)]
    n = 0
    for o in range(2):
        for cb in range(4):
            cf = o * 4096 + cb * 512
            for i in range(16):
                b = i % 2
                ts_ = slice(i * 128, (i + 1) * 128)
                P.dma("sp", win[b][:], C.c_win[ts_, cb * 512:(cb + 1) * 512], writes=[win[b]])
                P.mm(pf[b][:], h3[:, ts_], w3b[:, cf:cf + 512], True, True, reads=[h3, w3b], writes=[pf[b]])
                P.mm(pbk[b][:], h3[:, ts_], w3b[:, cf + 2048:cf + 2560], True, True, reads=[h3, w3b], writes=[pbk[b]])
                P.X("dve", "tensor_tensor", [pf[b], win[b]], [fw[b]], out=fw[b][:], in0=pf[b][:], in1=win[b][:], op=ALU.mult)
                P.X("dve", "tensor_tensor", [pbk[b], win[b]], [bw[b]], out=bw[b][:], in0=pbk[b][:], in1=win[b][:], op=ALU.mult)
                if i == 0:
                    P.X("dve", "memset", [], [bw[b]], ap=bw[b][0:1, :], constant=0.0)
                P.X("pool", "tensor_tensor", [fw[b], bw[b]], [sumt.k(i)], out=sumt[:, i, :], in0=fw[b][:], in1=bw[b][:], op=ALU.add)
                P.X("pool", "tensor_tensor", [fw[b], bw[b]], [dift.k(i)], out=dift[:, i, :], in0=bw[b][:], in1=fw[b][:], op=ALU.subtract)
                P.act(s1[b][:], fw[b][:], AF.Square, reads=[fw[b]], writes=[s1[b]])
                P.act(s2[b][:], bw[b][:], AF.Square, reads=[bw[b]], writes=[s2[b]])
                if i == 0:
                    P.X("dve", "tensor_tensor", [s1[b], s2[b]], [nacc], out=nacc[:], in0=s1[b][:], in1=s2[b][:], op=ALU.add)
                else:
                    P.X("dve", "tensor_tensor", [s1[b], nacc], [nacc], out=nacc[:], in0=s1[b][:], in1=nacc[:], op=ALU.add)
                    P.X("dve", "tensor_tensor", [s2[b], nacc], [nacc], out=nacc[:], in0=s2[b][:], in1=nacc[:], op=ALU.add)
            P.mm(psn[:], ones_f[:], nacc[:], True, True, reads=[ones_f, nacc], writes=[psn])
            P.act(rs2[:], psn[:], AF.Sqrt, reads=[psn], writes=[rs2])
            P.X("dve", "reciprocal", [rs2], [rs2], out=rs2[:], in_=rs2[:])
            P.X("dve", "tensor_scalar", [rs2], [rsN], out=rsN[:], in0=rs2[:], scalar1=1.0 / NFFT, scalar2=None, op0=ALU.mult)
            P.X("dve", "tensor_scalar", [rs2], [rs2], out=rs2[:], in0=rs2[:], scalar1=2.0 / NFFT, scalar2=None, op0=ALU.mult)
            skeys = [sumt.k(i) for i in range(16)]
            dkeys = [dift.k(i) for i in range(16)]
            for j in range(16):
                cmj = cmb[j % 3]
                smj = smb[j % 3]
                P.dma(CSQ, cmj[:], C.c_cm[j], writes=[cmj])
                P.dma(CSQ, smj[:], C.c_sm[j], writes=[smj])
                pP = pf[j % 2]
                pQ = pbk[j % 2]
                for k in range(16):
                    P.mm(pP[:], cmj[:, k, :], sumt[:, k, :], k == 0, k == 15, reads=[cmj, sumt.k(k)], writes=[pP])
                for k in range(16):
                    P.mm(pQ[:], smj[:, k, :], dift[:, k, :], k == 0, k == 15, reads=[smj, dift.k(k)], writes=[pQ])
                oP = po[n % 4]
                oQ = po[(n + 1) % 4]
                n += 2
                P.X("dve", "tensor_tensor", [pP, rs2], [oP], out=oP[:], in0=pP[:], in1=rs2[:], op=ALU.mult)
                if j == 0:
                    P.X("dve", "tensor_scalar", [oP], [oP], out=oP[0:1, :], in0=oP[0:1, :], scalar1=0.5, scalar2=None, op0=ALU.mult)
                P.X("dve", "tensor_tensor", [pQ, rs2], [oQ], out=oQ[:], in0=pQ[:], in1=rs2[:], op=ALU.mult)
                P.dma(STQ, S.PF[o, 0, j * 128:(j + 1) * 128, cb * 512:(cb + 1) * 512], oP[:], reads=[oP], writes=[("PF", o, 0, j, cb)])
                P.dma(STQ, S.PF[o, 1, j * 128:(j + 1) * 128, cb * 512:(cb + 1) * 512], oQ[:], reads=[oQ], writes=[("PF", o, 1, j, cb)])
            for k in range(16):
                P.mm(pnq[:], altT[:], sumt[:, k, :], k == 0, k == 15, reads=[altT, sumt.k(k)], writes=[pnq])
            oN = po[n % 4]
            n += 1
            P.X("dve", "tensor_tensor", [pnq, rsN], [oN], out=oN[:], in0=pnq[:], in1=rsN[:], op=ALU.mult)
            P.dma(STQ, S.PF[o, 0, 2048:2176, cb * 512:(cb + 1) * 512], oN[:], reads=[oN], writes=[("PF", o, 0, 16, cb)])
    P.end()


def phase_HC(P, C, l, t0):
    S = C.S
    P.begin()
    ident_f = P.sb("ident_f", [128, 128], F32)
    ident_b = P.sb("ident_b", [128, 128], BF16)
    altT = P.sb("altT", [128, 128], BF16)
    nyqT = P.sb("nyqT", [128, 128], BF16)
    P.dma("sp", ident_f[:], C.c_ident, writes=[ident_f])
    P.X("dve", "tensor_copy", [ident_f], [ident_b], out=ident_b[:], in_=ident_f[:])
    P.dma("sp", altT[:], C.c_alt, writes=[altT])
    P.dma("sp", nyqT[:], C.c_nyq, writes=[nyqT])
    vz = P.sb("vz", [128, 16, 512], BF16)
    zz = P.sb("zz", [128, 16, 512], BF16)
    YR = P.sb("YR", [128, 16, 512], BF16)
    YW = P.sb("YW", [128, 16, 512], BF16)
    YN = P.sb("YN", [128, 512], BF16)
    cmb = [P.sb("cmb", [128, 16, 128], BF16) for _ in range(3)]
    smb = [P.sb("smb", [128, 16, 128], BF16) for _ in range(3)]
    pq = [P.sb("pq", [128, 2, 512], F32) for _ in range(2)]
    cwt = [P.sb("cwt", [128, 3, 512], F32) for _ in range(3)]
    cbt = [P.sb("cbt", [128, 512], F32) for _ in range(3)]
    skt = [P.sb("skt", [128, 512], F32) for _ in range(2)]
    xs = [[P.sb("xs", [128, 512], F32) for _ in range(3)] for _ in range(3)]
    ca = [P.sb("ca", [128, 512], F32) for _ in range(2)]
    ct_ = [P.sb("ct", [128, 512], F32) for _ in range(2)]
    tt = [P.sb("tt", [128, 512], F32) for _ in range(4)]
    e1 = [P.sb("e1", [128, 512], F32) for _ in range(2)]
    e2 = [P.sb("e2", [128, 512], F32) for _ in range(2)]
    yb = [P.sb("yb", [128, 512], BF16) for _ in range(2)]
    yT4 = [P.sb("yT4", [128, 4, 128], BF16) for _ in range(2)]
    pA = [P.ps("pA", [128, 512], F32) for _ in range(2)]
    pB = [P.ps("pB", [128, 512], F32) for _ in range(2)]
    pY = [P.ps("pY", [128, 512], F32) for _ in range(2)]
    pN = P.ps("pN", [128, 512], F32)
    pT = P.ps("pT", [128, 512], BF16)
    cnt = {"x": 0, "c": 0, "cs": 0}

    def short_conv(part, cb, i, out_t):
        st = xs[cnt["x"] % 3]
        cnt["x"] += 1
        c0 = part * 2048 + cb * 512
        r0 = t0 + i * 128
        if i == 0:
            P.X("dve", "memset", [], [st[0]], ap=st[0][0:32, :], constant=0.0)
            P.dma("sp", st[0][1:128, :], S.HY[r0:r0 + 127, c0:c0 + 512], writes=[st[0]])
        else:
            P.dma("sp", st[0][:], S.HY[r0 - 1:r0 + 127, c0:c0 + 512], writes=[st[0]])
        P.dma("sp", st[1][:], S.HY[r0:r0 + 128, c0:c0 + 512], writes=[st[1]])
        if i == 15:
            P.X("dve", "memset", [], [st[2]], ap=st[2][96:128, :], constant=0.0)
            P.dma("sp", st[2][0:127, :], S.HY[r0 + 1:r0 + 128, c0:c0 + 512], writes=[st[2]])
        else:
            P.dma("sp", st[2][:], S.HY[r0 + 1:r0 + 129, c0:c0 + 512], writes=[st[2]])
        w = cwt[part]
        a = ca[cnt["c"] % 2]
        t = ct_[cnt["c"] % 2]
        cnt["c"] += 1
        P.X("dve", "tensor_tensor", [st[0], w], [a], out=a[:], in0=st[0][:], in1=w[:, 0, :], op=ALU.mult)
        P.X(PE2, "tensor_tensor", [st[1], w], [t], out=t[:], in0=st[1][:], in1=w[:, 1, :], op=ALU.mult)
        P.X("dve", "tensor_tensor", [a, t], [a], out=a[:], in0=a[:], in1=t[:], op=ALU.add)
        P.X(PE2, "tensor_tensor", [st[2], w], [t], out=t[:], in0=st[2][:], in1=w[:, 2, :], op=ALU.mult)
        P.X(PE2, "tensor_tensor", [a, cbt[part]], [a], out=a[:], in0=a[:], in1=cbt[part][:], op=ALU.add)
        P.X("dve", "tensor_tensor", [a, t], [out_t], out=out_t[:], in0=a[:], in1=t[:], op=ALU.add)

    def load_cs(j):
        b = cnt["cs"] % 3
        cnt["cs"] += 1
        if NOCS and cnt["cs"] > 3:
            return cmb[b], smb[b]
        P.dma(CSQ, cmb[b][:], C.c_cm[j], writes=[cmb[b]])
        P.dma(CSQ, smb[b][:], C.c_sm[j], writes=[smb[b]])
        return cmb[b], smb[b]

    def forward(o, cb, zin):
        zkeys = [zin.k(k) for k in range(16)]
        for j in range(16):
            cmj, smj = load_cs(j)
            f = pq[j % 2]
            P.dma("sp", f[:, 0, :], S.PF[o, 0, j * 128:(j + 1) * 128, cb * 512:(cb + 1) * 512], writes=[f.k(0)])
            P.dma("sp", f[:, 1, :], S.PF[o, 1, j * 128:(j + 1) * 128, cb * 512:(cb + 1) * 512], writes=[f.k(1)])
            a = pA[j % 2]
            b = pB[j % 2]
            for k in range(16):
                P.mm(a[:], cmj[:, k, :], zin[:, k, :], k == 0, k == 15, reads=[cmj, zin.k(k)], writes=[a])
            for k in range(16):
                P.mm(b[:], smj[:, k, :], zin[:, k, :], k == 0, k == 15, reads=[smj, zin.k(k)], writes=[b])
            fk = [f.k(0), f.k(1)]
            P.X("dve", "tensor_tensor", [a] + fk, [tt[0]], out=tt[0][:], in0=a[:], in1=f[:, 0, :], op=ALU.mult)
            P.X("dve", "tensor_tensor", [b] + fk, [tt[1]], out=tt[1][:], in0=b[:], in1=f[:, 1, :], op=ALU.mult)
            P.X("dve", "tensor_tensor", [b] + fk, [tt[2]], out=tt[2][:], in0=b[:], in1=f[:, 0, :], op=ALU.mult)
            P.X("dve", "tensor_tensor", [a] + fk, [tt[3]], out=tt[3][:], in0=a[:], in1=f[:, 1, :], op=ALU.mult)
            P.X(PE2, "tensor_tensor", [tt[0], tt[1]], [YR.k(j)], out=YR[:, j, :], in0=tt[0][:], in1=tt[1][:], op=ALU.add)
            P.X(PE2, "tensor_tensor", [tt[2], tt[3]], [YW.k(j)], out=YW[:, j, :], in0=tt[2][:], in1=tt[3][:], op=ALU.subtract)
        f = pq[0]
        P.dma("sp", f[:, 0, :], S.PF[o, 0, 2048:2176, cb * 512:(cb + 1) * 512], writes=[f.k(0)])
        for k in range(16):
            P.mm(pN[:], altT[:], zin[:, k, :], k == 0, k == 15, reads=[altT, zin.k(k)], writes=[pN])
        P.X("dve", "tensor_tensor", [pN, f.k(0)], [YN], out=YN[:], in0=pN[:], in1=f[:, 0, :], op=ALU.mult)

    def inverse(epi):
        for i in range(16):
            cmi, smi = load_cs(i)
            y = pY[i % 2]
            for k in range(16):
                P.mm(y[:], cmi[:, k, :], YR[:, k, :], k == 0, False, reads=[cmi, YR.k(k)], writes=[y])
            for k in range(16):
                P.mm(y[:], smi[:, k, :], YW[:, k, :], False, False, reads=[smi, YW.k(k)], writes=[y])
            P.mm(y[:], nyqT[:], YN[:], False, True, reads=[nyqT, YN], writes=[y])
            epi(i, y)

    def load_w(part, cb):
        P.dma("sp", cwt[part][:], C.hy_cw[l][:, part * 2048 + cb * 512:part * 2048 + (cb + 1) * 512].unsqueeze(0).to_broadcast([128, 3, 512]),
              writes=[cwt[part]])
        P.dma("sp", cbt[part][:], C.hy_cb[l][:, part * 2048 + cb * 512:part * 2048 + (cb + 1) * 512].to_broadcast([128, 512]),
              writes=[cbt[part]])

    def vpath(cb):
        load_w(2, cb)
        for i in range(16):
            e = e1[i % 2]
            short_conv(2, cb, i, e)
            P.dma(STQ, S.VC[cb % 2, i * 128:(i + 1) * 128, :], e[:], reads=[e], writes=[("VC", cb % 2, i)])
            P.op("act", lambda en, e=e, i=i: en.copy(out=vz[:, i, :], in_=e[:]), [e], [vz.k(i)])

    vpath(0)
    for cb in range(4):
        for part in range(2):
            load_w(part, cb)
        for o in range(2):
            P.dma("sp", skt[o][:], C.hy_skip[l][o:o + 1, cb * 512:(cb + 1) * 512].to_broadcast([128, 512]), writes=[skt[o]])
        forward(0, cb, vz)
        if cb + 1 < 4:
            vpath(cb + 1)

        def epi1(i, y, cb=cb):
            vc = e1[i % 2]
            x1 = e2[i % 2]
            P.dma("sp", vc[:], S.VC[cb % 2, i * 128:(i + 1) * 128, :], reads=[("VC", cb % 2, i)], writes=[vc])
            short_conv(0, cb, i, x1)
            P.X(PE2, "tensor_tensor", [vc, skt[0]], [vc], out=vc[:], in0=vc[:], in1=skt[0][:], op=ALU.mult)
            P.X("dve", "tensor_tensor", [y, vc], [vc], out=vc[:], in0=y[:], in1=vc[:], op=ALU.add)
            P.X(PE2, "tensor_tensor", [vc, x1], [vc], out=vc[:], in0=vc[:], in1=x1[:], op=ALU.mult)
            P.dma(STQ, S.ZC[i * 128:(i + 1) * 128, :], vc[:], reads=[vc], writes=[("ZC", i)])
            P.op("act", lambda en, vc=vc, i=i: en.copy(out=zz[:, i, :], in_=vc[:]), [vc], [zz.k(i)])

        inverse(epi1)
        forward(1, cb, zz)

        def epi2(i, y, cb=cb):
            zc = e1[i % 2]
            x2 = e2[i % 2]
            ybt = yb[i % 2]
            P.dma("sp", zc[:], S.ZC[i * 128:(i + 1) * 128, :], reads=[("ZC", i)], writes=[zc])
            short_conv(1, cb, i, x2)
            P.X(PE2, "tensor_tensor", [zc, skt[1]], [zc], out=zc[:], in0=zc[:], in1=skt[1][:], op=ALU.mult)
            P.X("dve", "tensor_tensor", [y, zc], [zc], out=zc[:], in0=y[:], in1=zc[:], op=ALU.add)
            P.X(PE2, "tensor_tensor", [zc, x2], [ybt], out=ybt[:], in0=zc[:], in1=x2[:], op=ALU.mult)
            for q in range(4):
                P.op("pe", lambda en, q=q, ybt=ybt: en.transpose(pT[:, q * 128:(q + 1) * 128], ybt[:, q * 128:(q + 1) * 128], ident_b[:]),
                     [ybt, ident_b], [pT])
            o4 = yT4[i % 2]
            evac(P, i, o4[:].rearrange("p q t -> p (q t)"), pT[:], [pT], [o4])
            P.dma(STQ, S.YT[cb * 512:(cb + 1) * 512, t0 + i * 128:t0 + (i + 1) * 128].rearrange("(q p) t -> p q t", p=128), o4[:],
                  reads=[o4], writes=[("YTa", cb, i)])

        inverse(epi2)
    P.end()


_CONSTS = None


def host_consts():
    global _CONSTS
    if _CONSTS is not None:
        return _CONSTS
    import ml_dtypes
    bf = ml_dtypes.bfloat16
    L = SEQ
    pos = np.arange(L, dtype=np.float32)
    t = pos / np.float32(L - 1)
    omega = (np.float32(2.0 * math.pi) * pos / np.float32(L)).astype(np.float32)
    bands = np.linspace(1e-4, 15.0, 16, dtype=np.float32)
    ang = (omega[:, None] * bands[None, :]).astype(np.float32)
    feat = np.concatenate([t[:, None], np.cos(ang), -np.sin(ang)], axis=-1).astype(np.float32)
    deltas = np.abs(np.linspace(math.log(1e-2) / 1.5, math.log(1e-2) / 0.3, 2048, dtype=np.float32))
    win = np.exp(-t[:, None] * deltas[None, :]).astype(np.float32)
    n = np.arange(2048, dtype=np.int64)
    prod = (n[:, None] * n[None, :]) % NFFT
    th = prod.astype(np.float64) * (2.0 * np.pi / NFFT)
    cm = np.cos(th)
    sm = np.sin(th)

    def blk(m):
        return np.ascontiguousarray(m.reshape(16, 128, 16, 128).transpose(2, 1, 0, 3)).astype(bf)

    alt = np.where(np.arange(128) % 2 == 0, 1.0, -1.0).astype(np.float32)
    altT = np.repeat(alt[:, None], 128, axis=1).astype(bf)
    nyq = np.zeros((128, 128), np.float32)
    nyq[0, :] = alt
    tri = np.zeros((64, 2, 64), np.float32)
    ii = np.arange(64)
    tri[:, 0, :] = (ii[:, None] <= ii[None, :])
    tri[:, 1, :] = (ii[:, None] >= ii[None, :])
    _CONSTS = {"c_featT": np.ascontiguousarray(feat.T), "c_win": win, "c_cm": blk(cm), "c_sm": blk(sm),
               "c_alt": altT, "c_nyq": nyq.astype(bf), "c_ident": np.eye(128, dtype=np.float32), "c_tri": tri}
    return _CONSTS


def add_inputs_C(C):
    L = C.L
    C.w_br = [C.inp("w_br_" + n, [L, DMIX, D]) for n in "abc"]
    C.w_out = C.inp("w_out", [L, D, D])
    C.ffn_norm = C.inp("ffn_norm", [L, 128, 32])
    C.w_gate = C.inp("w_gate", [L, D, FFN])
    C.w_up = C.inp("w_up", [L, D, FFN])
    C.w_down = C.inp("w_down", [L, FFN, D])
    C.final_norm = C.inp("final_norm", [1, 128, 32])


def phase_C1(P, C, l, t0):
    S = C.S
    NBC = 256
    P.begin()
    yT = P.sb("yT", [128, 48, 512], BF16)
    yv = S.YT[:, t0:t0 + 512].rearrange("(k p) t -> p k t", p=128)
    for a in range(0, 48, 8):
        P.dma("sp", yT[:, a:a + 8, :], yv[:, a:a + 8, :], writes=[yT.k(a // 8)])
    wb = [[P.sb("wbr", [128, 16, NBC], BF16) for _ in range(2)] for _ in range(3)]
    gt = [P.sb("gt", [128, 3, 512], F32) for _ in range(2)]
    m = [P.sb("m", [128, 512], F32) for _ in range(2)]
    t = [P.sb("t", [128, 512], F32) for _ in range(2)]
    mo = [P.sb("mo", [128, 512], BF16) for _ in range(2)]
    ps = [[P.ps("psb", [128, 512], F32) for _ in range(2)] for _ in range(3)]
    wvs = [C.w_br[br][l].rearrange("(k p) n -> p k n", p=128) for br in range(3)]
    gv = S.GT[:, t0:t0 + 512].rearrange("(b x p) t -> p b x t", b=3, p=128)
    it = 0
    for db in range(D // NBC):
        bufs = []
        for br in range(3):
            b = wb[br][db % 2]
            for a in range(0, 16, 8):
                P.dma("pool", b[:, a:a + 8, :], wvs[br][:, a:a + 8, db * NBC:(db + 1) * NBC], writes=[b.k(a // 8)])
            bufs.append(b)
        for nt in range(NBC // 128):
            dt = db * (NBC // 128) + nt
            g = gt[it % 2]
            P.dma("sp", g[:], gv[:, :, dt, :], writes=[g])
            for br in range(3):
                p_ = ps[br][it % 2]
                for k in range(16):
                    P.mm(p_[:], bufs[br][:, k, nt * 128:(nt + 1) * 128], yT[:, br * 16 + k, :], k == 0, k == 15,
                         reads=[bufs[br].k(k // 8), yT.k((br * 16 + k) // 8)], writes=[p_])
            mm_, tt_, oo_ = m[it % 2], t[it % 2], mo[it % 2]
            P.X("dve", "tensor_tensor", [ps[0][it % 2], g], [mm_], out=mm_[:], in0=ps[0][it % 2][:], in1=g[:, 0, :], op=ALU.mult)
            P.X("dve", "tensor_tensor", [ps[1][it % 2], g], [tt_], out=tt_[:], in0=ps[1][it % 2][:], in1=g[:, 1, :], op=ALU.mult)
            P.X("dve", "tensor_tensor", [mm_, tt_], [mm_], out=mm_[:], in0=mm_[:], in1=tt_[:], op=ALU.add)
            P.X("dve", "tensor_tensor", [ps[2][it % 2], g], [tt_], out=tt_[:], in0=ps[2][it % 2][:], in1=g[:, 2, :], op=ALU.mult)
            P.X("dve", "tensor_tensor", [mm_, tt_], [oo_], out=oo_[:], in0=mm_[:], in1=tt_[:], op=ALU.add)
            P.dma("act", S.MT[dt * 128:(dt + 1) * 128, t0:t0 + 512], oo_[:], reads=[oo_], writes=[("MT", dt)])
            it += 1
    P.end()


def gemm_residual(P, C, G, xT, xkeys, kc, wv, k0, t0, piece=8):
    S = C.S
    xt = [P.sb("xt", [128, 512], F32) for _ in range(4)]
    cnt = [0]

    def epi(col0, n, tb, ps):
        x = xt[cnt[0] % 4]
        cnt[0] += 1
        dst = S.XT[col0:col0 + n, t0 + tb * 512:t0 + (tb + 1) * 512]
        P.dma("sp", x[0:n, :], dst, reads=[("XT", col0, tb)], writes=[x])
        P.X("dve", "tensor_tensor", [ps, x], [x], out=x[0:n, :], in0=ps[0:n, :], in1=x[0:n, :], op=ALU.add)
        P.dma("act", dst, x[0:n, :], reads=[x], writes=[("XT", col0, tb)])

    G.run_fm(xT, xkeys, kc, 1024, wv, k0, 0, D, epi, piece=piece)


def phase_C2(P, C, l, t0):
    S = C.S
    P.begin()
    mT = P.sb("mT", [128, 32, 1024], BF16)
    mv = S.MT[:, t0:t0 + 1024].rearrange("(k p) t -> p k t", p=128)
    for a in range(0, 32, 8):
        P.dma("sp", mT[:, a:a + 8, :], mv[:, a:a + 8, :], writes=[mT.k(a // 8)])
    G = Gemm(P, 32, 512, nbuf=2, npsum=4)
    wv = C.w_out[l].rearrange("(k p) n -> p k n", p=128)
    gemm_residual(P, C, G, mT, [mT.k(a) for a in range(4)], 32, wv, 0, t0)
    P.end()


def phase_C3(P, C, l, t0):
    S = C.S
    NBC = 256
    P.begin()
    ones_f = P.sb("ones_f", [128, 128], F32)
    P.X("dve", "memset", [], [ones_f], ap=ones_f[:], constant=1.0)
    C.epsb = P.sb("epsb", [128, 1], F32)
    P.X("dve", "memset", [], [C.epsb], ap=C.epsb[:], constant=EPS)
    wn = P.sb("wn", [128, 32], F32)
    P.dma("sp", wn[:], C.ffn_norm[l], writes=[wn])
    uT = P.sb("uT", [128, 32, 1024], BF16)
    ukeys = [uT.k(k) for k in range(32)]
    banks = [P.ps("bk", [128, 512], F32) for _ in range(8)]
    rms_build(P, C, S.XT, t0, wn, uT, ones_f, pst=banks[0:2])
    wg = [P.sb("wg", [128, 32, NBC], BF16) for _ in range(2)]
    wu = [P.sb("wu", [128, 32, NBC], BF16) for _ in range(2)]
    sg = [P.sb("sg", [128, 512], F32) for _ in range(2)]
    fo = [P.sb("fo", [128, 512], BF16) for _ in range(4)]
    gv = C.w_gate[l].rearrange("(k p) n -> p k n", p=128)
    uv = C.w_up[l].rearrange("(k p) n -> p k n", p=128)
    it = 0
    for fb in range(FFN // NBC):
        bg, bu = wg[fb % 2], wu[fb % 2]
        for a in range(0, 32, 8):
            P.dma("pool", bg[:, a:a + 8, :], gv[:, a:a + 8, fb * NBC:(fb + 1) * NBC], writes=[bg.k(a // 8)])
            P.dma("pool", bu[:, a:a + 8, :], uv[:, a:a + 8, fb * NBC:(fb + 1) * NBC], writes=[bu.k(a // 8)])
        for nt in range(NBC // 128):
            f0 = fb * NBC + nt * 128
            pg = [banks[(it % 2) * 4 + tb] for tb in range(2)]
            pu = [banks[(it % 2) * 4 + 2 + tb] for tb in range(2)]
            for k in range(32):
                for tb in range(2):
                    P.mm(pg[tb][:], bg[:, k, nt * 128:(nt + 1) * 128], uT[:, k, tb * 512:(tb + 1) * 512], k == 0, k == 31,
                         reads=[bg.k(k // 8), uT.k(k)], writes=[pg[tb]])
            for k in range(32):
                for tb in range(2):
                    P.mm(pu[tb][:], bu[:, k, nt * 128:(nt + 1) * 128], uT[:, k, tb * 512:(tb + 1) * 512], k == 0, k == 31,
                         reads=[bu.k(k // 8), uT.k(k)], writes=[pu[tb]])
            for tb in range(2):
                s = sg[tb]
                o = fo[(it % 2) * 2 + tb]
                P.act(s[:], pg[tb][:], AF.Silu, reads=[pg[tb]], writes=[s])
                P.X("dve", "tensor_tensor", [pu[tb], s], [o], out=o[:], in0=pu[tb][:], in1=s[:], op=ALU.mult)
                P.dma("sp", S.FF[f0:f0 + 128, t0 + tb * 512:t0 + (tb + 1) * 512], o[:], reads=[o], writes=[("FF", f0, tb)])
            it += 1
    P.end()


def phase_C4(P, C, l, t0):
    S = C.S
    KH = FFN // 128 // 2
    for kh in range(2):
        P.begin()
        fT = P.sb("fT", [128, KH, 1024], BF16)
        fv = S.FF[kh * KH * 128:(kh + 1) * KH * 128, t0:t0 + 1024].rearrange("(k p) t -> p k t", p=128)
        pieces = list(range(0, KH, 8))
        for a in pieces:
            b = min(KH, a + 8)
            P.dma("sp", fT[:, a:b, :], fv[:, a:b, :], writes=[fT.k(a // 8)])
        G = Gemm(P, KH, 256, nbuf=2, npsum=4)
        wv = C.w_down[l].rearrange("(k p) n -> p k n", p=128)
        gemm_residual(P, C, G, fT, [fT.k(a // 8) for a in pieces], KH, wv, kh * KH, t0)
        P.end()


def phase_F(P, C, t0):
    S = C.S
    P.begin()
    ones_f = P.sb("ones_f", [128, 128], F32)
    P.X("dve", "memset", [], [ones_f], ap=ones_f[:], constant=1.0)
    epsb = P.sb("epsb", [128, 1], F32)
    P.X("dve", "memset", [], [epsb], ap=epsb[:], constant=EPS)
    wn = P.sb("wn", [128, 32], F32)
    P.dma("sp", wn[:], C.final_norm[0], writes=[wn])
    xk = [P.sb("xk", [128, 1024], F32) for _ in range(3)]
    sq = [P.sb("sq", [128, 1024], F32) for _ in range(2)]
    oo = [P.sb("oo", [128, 1024], F32) for _ in range(2)]
    rstd = P.sb("rstd", [128, 1024], F32)
    pst = [P.ps("pst", [128, 512], F32) for _ in range(2)]
    for k in range(32):
        x = xk[k % 3]
        s = sq[k % 2]
        P.dma("sp", x[:], S.XT[k * 128:(k + 1) * 128, t0:t0 + 1024], writes=[x])
        P.act(s[:], x[:], AF.Square, reads=[x], writes=[s])
        for tb in range(2):
            P.mm(pst[tb][:], ones_f[:], s[:, tb * 512:(tb + 1) * 512], k == 0, k == 31, reads=[s, ones_f], writes=[pst[tb]])
    for tb in range(2):
        sl = slice(tb * 512, (tb + 1) * 512)
        P.act(rstd[:, sl], pst[tb][:], AF.Sqrt, reads=[pst[tb], epsb], writes=[rstd.k(tb)], bias=epsb[:, 0:1], scale=1.0 / D)
        P.X("dve", "reciprocal", [rstd.k(tb)], [rstd.k(tb)], out=rstd[:, sl], in_=rstd[:, sl])
    for k in range(32):
        x = xk[k % 3]
        o = oo[k % 2]
        P.dma("sp", x[:], S.XT[k * 128:(k + 1) * 128, t0:t0 + 1024], writes=[x])
        P.X("dve", "scalar_tensor_tensor", [x, wn, rstd.k(0), rstd.k(1)], [o], out=o[:], in0=x[:], scalar=wn[:, k:k + 1],
            in1=rstd[:], op0=ALU.mult, op1=ALU.mult)
        P.dma("act", C.OUT[k * 128:(k + 1) * 128, t0:t0 + 1024], o[:], reads=[o], writes=[("OUT", k)])
    P.end()


NCORES = 4
NT_CORE = SEQ * (4 // NCORES)
_PROG = {}


def build_full(nt=NT_CORE, depth=DEPTH):
    key = (nt, depth)
    if key in _PROG:
        return _PROG[key]
    nc = bass.Bass("TRN2", target_bir_lowering=False)
    C = make_ctx(nc, nt, depth)
    C.x_T = C.inp("x_T", [D, nt])
    C.OUT = dram(nc, "OUT", [D, nt], F32, "ExternalOutput")
    add_inputs_A(C)
    add_inputs_H(C)
    add_inputs_M(C)
    add_inputs_R(C)
    add_inputs_C(C)
    P = Prog(nc)
    S = C.S
    P.begin()
    for k in range(8):
        P.dma("sp", S.XT[k * 512:(k + 1) * 512, :], C.x_T[k * 512:(k + 1) * 512, :], writes=[("XTinit", k)])
    P.end()
    nb = nt // SEQ
    for l in range(depth):
        for t0 in range(0, nt, 1024):
            phase_A(P, C, l, t0)
        phase_HF(P, C, l)
        for b in range(nb):
            phase_HC(P, C, l, b * SEQ)
            phase_M(P, C, l, b * SEQ)
            phase_R(P, C, l, b * SEQ)
        for t0 in range(0, nt, 1024):
            phase_C1(P, C, l, t0)
            phase_C1(P, C, l, t0 + 512)
            phase_C2(P, C, l, t0)
            phase_C3(P, C, l, t0)
            phase_C4(P, C, l, t0)
    for t0 in range(0, nt, 1024):
        phase_F(P, C, t0)
    P.begin()
    P.end(final=True)
    _PROG[key] = nc
    return nc


def _pk(v):
    lead = v.shape[:-1]
    n = v.shape[-1] // 128
    return np.ascontiguousarray(np.moveaxis(v.reshape(lead + (n, 128)), -1, -2))


def host_params(inp):
    f = lambda a: np.ascontiguousarray(np.asarray(a, dtype=np.float32))
    L = inp["w_in"].shape[0]
    p = {}
    p["mix_norm"] = _pk(f(inp["mix_norm"]))
    p["w_in"] = f(inp["w_in"])
    p["b_gate"] = _pk(f(inp["b_gate"]))
    p["hy_w1"] = f(inp["hy_w1"])
    p["hy_b1"] = f(inp["hy_b1"]).reshape(L, 64, 1)
    p["hy_w2"] = f(inp["hy_w2"])
    p["hy_b2"] = np.ascontiguousarray(f(inp["hy_b2"]).transpose(0, 2, 1))
    p["hy_freq"] = f(inp["hy_freq"]).reshape(L, 64, 1)
    p["hy_w3"] = f(inp["hy_w3"])
    p["hy_cw"] = f(inp["hy_conv_w"])
    p["hy_cb"] = f(inp["hy_conv_b"]).reshape(L, 1, 6144)
    p["hy_skip"] = f(inp["hy_skip"])
    p["ml_gb"] = f(inp["ml_gate_b"]).reshape(L, 1, 32)
    p["ml_norm"] = f(inp["ml_norm"]).reshape(L, 1, 2048)
    p["rg_cw"] = np.ascontiguousarray(f(inp["rg_conv_w"]).reshape(L, 4, 16, 128).transpose(0, 3, 2, 1))
    p["rg_cb"] = _pk(f(inp["rg_conv_b"]))
    for a, b in (("rg_ba", "rg_ba"), ("rg_bx", "rg_bx"), ("rg_lam", "rg_lambda")):
        p[a] = np.ascontiguousarray(f(inp[b]).reshape(L, 2, 16, 128).transpose(0, 3, 1, 2))
    p["rg_wa"] = f(inp["rg_wa"])
    p["rg_wx"] = f(inp["rg_wx"])
    p["w_br_a"] = f(inp["w_br_a"])
    p["w_br_b"] = f(inp["w_br_b"])
    p["w_br_c"] = f(inp["w_br_c"])
    p["w_out"] = f(inp["w_out"])
    p["ffn_norm"] = _pk(f(inp["ffn_norm"]))
    p["w_gate"] = f(inp["w_gate"])
    p["w_up"] = f(inp["w_up"])
    p["w_down"] = f(inp["w_down"])
    p["final_norm"] = _pk(f(inp["final_norm"]).reshape(1, D))
    p.update(host_consts())
    return p


def kernel(**inputs):
    x = np.asarray(inputs["x"], dtype=np.float32)
    nc = build_full()
    params = host_params(inputs)
    bpc = 4 // NCORES
    in_maps = []
    for c in range(NCORES):
        xs = x[c * bpc:(c + 1) * bpc].reshape(bpc * SEQ, D)
        m = dict(params)
        m["x_T"] = np.ascontiguousarray(xs.T)
        in_maps.append(m)
    res = run_bass_kernel_spmd(nc, in_maps, core_ids=list(range(NCORES)))
    outs = [np.ascontiguousarray(r["OUT"].T).reshape(bpc, SEQ, D) for r in res.results]
    return np.concatenate(outs, axis=0).astype(np.float32)
```
